# Optimizing a Trainium2 kernel written in Bass

```python
import math
import jax, jax.numpy as jnp
from jax import lax
import numpy as np

D_MODEL = 4096
BATCH = 1
SEQ = 16384
DEPTH = 4

HEAD_DIM = 128
N_MIXERS = 2
NSA_HEADS = D_MODEL // HEAD_DIM
NSA_KV_GROUPS = 4
NSA_HG = NSA_HEADS // NSA_KV_GROUPS
NSA_KV_DIM = NSA_KV_GROUPS * HEAD_DIM
CMP_BLOCK = 32
CMP_STRIDE = 16
SEL_BLOCK = 64
SEL_TOPK = 16
WINDOW = 512
NSA_IN = NSA_HEADS * HEAD_DIM + 6 * NSA_KV_DIM + 3 * NSA_HEADS
FOX_HEADS = D_MODEL // HEAD_DIM
FOX_IN = 3 * FOX_HEADS * HEAD_DIM + FOX_HEADS
D_FF = -(-8 * D_MODEL // (3 * 256)) * 256
REL_BUCKETS = 32
REL_MAX_DIST = 128
Q_BLOCK = 128
N_NSA_LAYERS = (DEPTH + 1) // 2
N_FOX_LAYERS = DEPTH // 2
RMS_EPS = 1e-6
NEG_INF = -1e30
FORCED_SCORE = 1e6

kernel_name = 'hybrid_nsa_fox_swiglu'


def rmsnorm(x, g):
    x32 = x.astype(jnp.float32)
    y = x32 * lax.rsqrt(jnp.mean(x32 * x32, axis=-1, keepdims=True) + RMS_EPS)
    return (y * g.astype(jnp.float32)).astype(x.dtype)


def rel_bucket(dist):
    n = jnp.maximum(dist, 0)
    max_exact = REL_BUCKETS // 2
    nf = jnp.maximum(n, max_exact).astype(jnp.float32)
    large = max_exact + (jnp.log(nf / max_exact) / math.log(REL_MAX_DIST / max_exact)
                         * (REL_BUCKETS - max_exact)).astype(jnp.int32)
    large = jnp.minimum(large, REL_BUCKETS - 1)
    return jnp.where(n < max_exact, n, large)


def masked_softmax(s, mask):
    return jax.nn.softmax(jnp.where(mask, s, NEG_INF), axis=-1)


def compress(k, pos, w1, w2):
    B, T, G, HD = k.shape
    kb = k.reshape(B, T // CMP_STRIDE, CMP_STRIDE, G, HD)
    blocks = jnp.concatenate([kb[:, :-1], kb[:, 1:]], axis=2)
    blocks = blocks + pos[None, None, :, None, :].astype(k.dtype)
    nc = blocks.shape[1]
    flat = blocks.transpose(0, 1, 3, 2, 4).reshape(B, nc, G, CMP_BLOCK * HD)
    return jax.nn.gelu(flat @ w1) @ w2


def nsa_mixer(h, w_in, w_o, cmp_pos, cmp_w1, cmp_w2, rel_table):
    B, T, _ = h.shape
    G, HG, HD = NSA_KV_GROUPS, NSA_HG, HEAD_DIM
    NQ = T // Q_BLOCK
    NS = T // SEL_BLOCK
    NC = T // CMP_STRIDE - 1
    k_top = min(SEL_TOPK, NS)
    scale = HD ** -0.5
    f32 = jnp.float32

    proj = h @ w_in
    qd = NSA_HEADS * HD
    q = proj[..., :qd].reshape(B, T, G, HG, HD)
    kv = proj[..., qd:qd + 6 * NSA_KV_DIM].reshape(B, T, 6, G, HD)
    k_c, v_c, k_s, v_s, k_w, v_w = [kv[:, :, j] for j in range(6)]
    gates = jax.nn.sigmoid(proj[..., qd + 6 * NSA_KV_DIM:].astype(f32)).reshape(B, T, G, HG, 3)

    kc = compress(k_c, cmp_pos[0], cmp_w1[0], cmp_w2[0])
    vc = compress(v_c, cmp_pos[1], cmp_w1[1], cmp_w2[1])
    c_start = jnp.arange(NC) * CMP_STRIDE
    cmp_end = c_start + CMP_BLOCK - 1
    s_start = jnp.arange(NS) * SEL_BLOCK
    overlap = ((c_start[:, None] < s_start[None, :] + SEL_BLOCK)
               & (c_start[:, None] + CMP_BLOCK > s_start[None, :])).astype(f32)

    ks_blk = k_s.reshape(B, NS, SEL_BLOCK, G, HD).transpose(0, 3, 1, 2, 4)
    vs_blk = v_s.reshape(B, NS, SEL_BLOCK, G, HD).transpose(0, 3, 1, 2, 4)
    b_ix = jnp.arange(B)[:, None, None, None]
    g_ix = jnp.arange(G)[None, None, :, None]
    table = rel_table.astype(f32)
    tab_g = table.reshape(REL_BUCKETS, G, HG).transpose(1, 0, 2)

    def head_bias(bkt):
        return table[bkt].reshape(bkt.shape + (G, HG)).transpose(2, 3, 0, 1)

    WL = WINDOW + Q_BLOCK
    kw_pad = jnp.pad(k_w, ((0, 0), (WINDOW, 0), (0, 0), (0, 0)))
    vw_pad = jnp.pad(v_w, ((0, 0), (WINDOW, 0), (0, 0), (0, 0)))
    win_rel = jnp.arange(Q_BLOCK)[:, None] - jnp.arange(WL)[None, :] + WINDOW
    win_bias = head_bias(rel_bucket(win_rel))
    win_band = (win_rel >= 0) & (win_rel < WINDOW)
    sel_off = jnp.arange(SEL_BLOCK)
    blk_ids = jnp.arange(NS)

    def block(i):
        t0 = i * Q_BLOCK
        tpos = t0 + jnp.arange(Q_BLOCK)
        qb = lax.dynamic_slice_in_dim(q, t0, Q_BLOCK, 1)
        gb = lax.dynamic_slice_in_dim(gates, t0, Q_BLOCK, 1)

        s_c = (jnp.einsum('btghd,bngd->bghtn', qb, kc).astype(f32) * scale
               + head_bias(rel_bucket(tpos[:, None] - cmp_end[None, :])))
        p_c = masked_softmax(s_c, cmp_end[None, :] <= tpos[:, None])
        p_c = p_c * (tpos >= CMP_BLOCK - 1).astype(f32)[:, None]
        o_c = jnp.einsum('bghtn,bngd->btghd', p_c.astype(vc.dtype), vc)

        imp = jnp.einsum('bghtn,ns->btgs', p_c, overlap)
        cur = tpos // SEL_BLOCK
        forced = ((blk_ids[None, :] == 0) | (blk_ids[None, :] == cur[:, None])
                  | (blk_ids[None, :] == cur[:, None] - 1))
        causal_blk = blk_ids[None, :] * SEL_BLOCK <= tpos[:, None]
        score = jnp.where(forced[None, :, None, :], FORCED_SCORE,
                          jnp.where(causal_blk[None, :, None, :], imp, -1.0))
        _, idx = lax.top_k(score, k_top)
        k_g = ks_blk[b_ix, g_ix, idx].reshape(B, Q_BLOCK, G, k_top * SEL_BLOCK, HD)
        v_g = vs_blk[b_ix, g_ix, idx].reshape(B, Q_BLOCK, G, k_top * SEL_BLOCK, HD)
        spos = (idx[..., None] * SEL_BLOCK + sel_off).reshape(B, Q_BLOCK, G, k_top * SEL_BLOCK)
        bias_s = tab_g[g_ix, rel_bucket(tpos[None, :, None, None] - spos)].transpose(0, 2, 4, 1, 3)
        s_s = jnp.einsum('btghd,btgkd->bghtk', qb, k_g).astype(f32) * scale + bias_s
        mask_s = (spos <= tpos[None, :, None, None]).transpose(0, 2, 1, 3)[:, :, None]
        p_s = masked_softmax(s_s, mask_s)
        o_s = jnp.einsum('bghtk,btgkd->btghd', p_s.astype(v_g.dtype), v_g)

        kwin = lax.dynamic_slice_in_dim(kw_pad, t0, WL, 1)
        vwin = lax.dynamic_slice_in_dim(vw_pad, t0, WL, 1)
        s_w = jnp.einsum('btghd,bsgd->bghts', qb, kwin).astype(f32) * scale + win_bias
        mask_w = win_band & ((t0 - WINDOW + jnp.arange(WL)) >= 0)[None, :]
        p_w = masked_softmax(s_w, mask_w)
        o_w = jnp.einsum('bghts,bsgd->btghd', p_w.astype(vwin.dtype), vwin)

        o = (gb[..., 0:1] * o_c.astype(f32) + gb[..., 1:2] * o_s.astype(f32)
             + gb[..., 2:3] * o_w.astype(f32))
        return o.astype(h.dtype)

    o = lax.map(block, jnp.arange(NQ))
    o = jnp.moveaxis(o, 0, 1).reshape(B, T, NSA_HEADS * HD)
    return o @ w_o


def fox_mixer(h, w_in, b_f, w_o):
    B, T, _ = h.shape
    H, HD = FOX_HEADS, HEAD_DIM
    NQ = T // Q_BLOCK
    scale = HD ** -0.5
    f32 = jnp.float32
    proj = h @ w_in
    qkv = proj[..., :3 * H * HD].reshape(B, T, 3, H, HD)
    q, k, v = qkv[:, :, 0], qkv[:, :, 1], qkv[:, :, 2]
    log_f = jax.nn.log_sigmoid(proj[..., 3 * H * HD:].astype(f32) + b_f.astype(f32))
    cum = jnp.cumsum(log_f, axis=1).transpose(0, 2, 1)
    kpos = jnp.arange(T)

    def block(i):
        t0 = i * Q_BLOCK
        tpos = t0 + jnp.arange(Q_BLOCK)
        qb = lax.dynamic_slice_in_dim(q, t0, Q_BLOCK, 1)
        cq = lax.dynamic_slice_in_dim(cum, t0, Q_BLOCK, 2)
        s = (jnp.einsum('bthd,bshd->bhts', qb, k).astype(f32) * scale
             + (cq[..., None] - cum[:, :, None, :]))
        p = masked_softmax(s, kpos[None, :] <= tpos[:, None])
        return jnp.einsum('bhts,bshd->bthd', p.astype(v.dtype), v)

    o = lax.map(block, jnp.arange(NQ))
    o = jnp.moveaxis(o, 0, 1).reshape(B, T, H * HD)
    return o @ w_o


def swiglu(h, w_gate, w_up, w_down):
    return (jax.nn.silu(h @ w_gate) * (h @ w_up)) @ w_down


def setup_inputs(seed: int = 0) -> dict:
    key = jax.random.key(seed)
    ks = jax.random.split(key, 16)
    f32 = jnp.float32

    def nrm(k, shape, fan):
        return jax.random.normal(k, shape, f32) * (fan ** -0.5)

    x = jax.random.normal(ks[0], (BATCH, SEQ, D_MODEL), f32)
    norm_mix = 1.0 + 0.01 * jax.random.normal(ks[1], (DEPTH, D_MODEL), f32)
    norm_ffn = 1.0 + 0.01 * jax.random.normal(ks[2], (DEPTH, D_MODEL), f32)
    norm_final = 1.0 + 0.01 * jax.random.normal(ks[3], (D_MODEL,), f32)
    rel_table = 0.5 * jax.random.normal(ks[4], (REL_BUCKETS, NSA_HEADS), f32)
    nsa_w_in = nrm(ks[5], (N_NSA_LAYERS, D_MODEL, NSA_IN), D_MODEL)
    nsa_w_o = nrm(ks[6], (N_NSA_LAYERS, NSA_HEADS * HEAD_DIM, D_MODEL), NSA_HEADS * HEAD_DIM)
    nsa_cmp_pos = 0.1 * jax.random.normal(ks[7], (N_NSA_LAYERS, 2, CMP_BLOCK, HEAD_DIM), f32)
    nsa_cmp_w1 = nrm(ks[8], (N_NSA_LAYERS, 2, CMP_BLOCK * HEAD_DIM, HEAD_DIM), CMP_BLOCK * HEAD_DIM)
    nsa_cmp_w2 = nrm(ks[9], (N_NSA_LAYERS, 2, HEAD_DIM, HEAD_DIM), HEAD_DIM)
    fox_w_in = nrm(ks[10], (N_FOX_LAYERS, D_MODEL, FOX_IN), D_MODEL)
    fox_b_f = jax.random.uniform(ks[11], (N_FOX_LAYERS, FOX_HEADS), f32, 1.0, 6.0)
    fox_w_o = nrm(ks[12], (N_FOX_LAYERS, FOX_HEADS * HEAD_DIM, D_MODEL), FOX_HEADS * HEAD_DIM)
    ffn_w_gate = nrm(ks[13], (DEPTH, D_MODEL, D_FF), D_MODEL)
    ffn_w_up = nrm(ks[14], (DEPTH, D_MODEL, D_FF), D_MODEL)
    ffn_w_down = nrm(ks[15], (DEPTH, D_FF, D_MODEL), D_FF)
    return {'x': x, 'norm_mix': norm_mix, 'norm_ffn': norm_ffn, 'norm_final': norm_final,
            'rel_table': rel_table, 'nsa_w_in': nsa_w_in, 'nsa_w_o': nsa_w_o,
            'nsa_cmp_pos': nsa_cmp_pos, 'nsa_cmp_w1': nsa_cmp_w1, 'nsa_cmp_w2': nsa_cmp_w2,
            'fox_w_in': fox_w_in, 'fox_b_f': fox_b_f, 'fox_w_o': fox_w_o,
            'ffn_w_gate': ffn_w_gate, 'ffn_w_up': ffn_w_up, 'ffn_w_down': ffn_w_down}


def reference(x, norm_mix, norm_ffn, norm_final, rel_table, nsa_w_in, nsa_w_o, nsa_cmp_pos,
              nsa_cmp_w1, nsa_cmp_w2, fox_w_in, fox_b_f, fox_w_o, ffn_w_gate, ffn_w_up, ffn_w_down):
    h = x
    for i in range(DEPTH):
        a = rmsnorm(h, norm_mix[i])
        j = i // N_MIXERS
        if i % N_MIXERS == 0:
            mix = nsa_mixer(a, nsa_w_in[j], nsa_w_o[j], nsa_cmp_pos[j], nsa_cmp_w1[j],
                            nsa_cmp_w2[j], rel_table)
        else:
            mix = fox_mixer(a, fox_w_in[j], fox_b_f[j], fox_w_o[j])
        h = h + mix
        h = h + swiglu(rmsnorm(h, norm_ffn[i]), ffn_w_gate[i], ffn_w_up[i], ffn_w_down[i])
    return rmsnorm(h, norm_final)
```

```python
import numpy as np
import ml_dtypes
from contextlib import ExitStack
import concourse.bass as bass
import concourse.mybir as mybir
from concourse.bass_utils import run_bass_kernel_spmd

F32 = mybir.dt.float32
BF16 = mybir.dt.bfloat16
ALU = mybir.AluOpType
AF = mybir.ActivationFunctionType
AX = mybir.AxisListType
NCORES = 8
N_DMA_SEMS = 6
RMS_EPS = 1e-6


class Buf:
    __slots__ = ("t", "last_w", "readers", "name")

    def __init__(self, t, name=""):
        self.t = t
        self.last_w = None
        self.readers = {}
        self.name = name

    def __getitem__(self, idx):
        return self.t[idx]

    def ap(self):
        return self.t.ap()


class _Eng:
    def __init__(self, name, eng, sem, same_sync):
        self.name, self.eng, self.sem, self.same_sync = name, eng, sem, same_sync
        self.count = 0
        self.waited = {}


class FW:
    def __init__(self, nc, stack):
        self.nc, self.stack = nc, stack
        self.sems, self.E = {}, {}
        for name, eng, same in (("pe", nc.tensor, False), ("act", nc.scalar, True),
                                ("dve", nc.vector, True), ("pool", nc.gpsimd, True),
                                ("sp", nc.sync, False)):
            sem = stack.enter_context(nc.semaphore("prog_" + name))
            self.sems["prog_" + name] = sem
            self.E[name] = _Eng(name, eng, sem, same)
        self.dq = {}
        for q in ("sp", "pool", "act"):
            lst = []
            for i in range(N_DMA_SEMS):
                key = f"dma_{q}_{i}"
                self.sems[key] = stack.enter_context(nc.semaphore(key))
                lst.append([key, 0])
            self.dq[q] = [lst, 0]
        self.sems["cc"] = stack.enter_context(nc.semaphore("cc"))
        self.cc_count = 0
        self.n_inst = 0
        self.rr = 0

    def sbuf(self, st, name, shape, dtype):
        self.uid = getattr(self, "uid", 0) + 1
        name = f"{name}_{self.uid}"
        return Buf(st.enter_context(self.nc.sbuf_tensor(name, shape, dtype)), name)

    def psum(self, st, name, shape, dtype=F32):
        self.uid = getattr(self, "uid", 0) + 1
        name = f"{name}_{self.uid}"
        return Buf(st.enter_context(self.nc.psum_tensor(name, shape, dtype)), name)

    def dram(self, name, shape, dtype):
        return Buf(self.nc.dram_tensor(name, shape, dtype, kind="Internal"), name)

    def _wait(self, e, tok):
        key, val = tok
        if key == "prog_" + e.name and not e.same_sync:
            return
        if e.waited.get(key, 0) >= val:
            return
        e.eng.wait_ge(self.sems[key], val)
        e.waited[key] = val

    def _deps(self, e, reads, writes):
        for r in reads:
            if r.last_w is not None:
                self._wait(e, r.last_w)
        for w in writes:
            if w.last_w is not None:
                self._wait(e, w.last_w)
            for tok in list(w.readers.items()):
                self._wait(e, tok)

    def _commit(self, tok, reads, writes):
        for r in reads:
            if r.readers.get(tok[0], 0) < tok[1]:
                r.readers[tok[0]] = tok[1]
        for w in writes:
            w.last_w = tok
            w.readers = {}

    def op(self, eng, fn, reads=(), writes=()):
        e = self.E[eng]
        self._deps(e, reads, writes)
        inst = fn(e.eng)
        e.count += 1
        inst.then_inc(e.sem, 1)
        self._commit(("prog_" + e.name, e.count), reads, writes)
        self.n_inst += 1

    def dma(self, q, out, in_, reads=(), writes=(), **kw):
        e = self.E[q]
        lst, idx = self.dq[q]
        slot = lst[idx % N_DMA_SEMS]
        self.dq[q][1] = idx + 1
        key, cnt = slot
        if cnt > 0:
            self._wait(e, (key, cnt))
        self._deps(e, reads, writes)
        inst = e.eng.dma_start(out=out, in_=in_, **kw)
        slot[1] = cnt + 16
        inst.then_inc(self.sems[key], 16)
        self._commit((key, cnt + 16), reads, writes)
        self.n_inst += 1

    def collective(self, kind, groups, src, dst):
        e = self.E["pool"]
        self._deps(e, [src], [dst])
        inst = self.nc.gpsimd.collective_compute(kind, ALU.bypass, replica_groups=groups,
                                                 ins=[src.ap().opt()], outs=[dst.ap().opt()])
        self.cc_count += 1
        inst.then_inc(self.sems["cc"])
        self._commit(("cc", self.cc_count), [src], [dst])
        self.n_inst += 1

    def allgather8(self, src, mid, dst):
        self.collective("AllGather", [[0, 1, 2, 3], [4, 5, 6, 7]], src, mid)
        self.collective("AllGather", [[0, 4], [1, 5], [2, 6], [3, 7]], mid, dst)


    def switch_dmas(self, per_core, reads, writes):
        e = self.E["sp"]
        k = len(per_core[0])
        slots = []
        lst = self.dq["sp"][0]
        for i in range(k):
            idx = self.dq["sp"][1]
            slot = lst[idx % N_DMA_SEMS]
            self.dq["sp"][1] = idx + 1
            if slot[1] > 0:
                self._wait(e, (slot[0], slot[1]))
            slots.append(slot)
        self._deps(e, reads, writes)
        if getattr(self, "_pid", None) is None:
            self._pid = self.nc.sync.partition_id()
        for c in self.nc.Switch(engines=[self.nc.sync], index=[self._pid], n=NCORES):
            incs = {}
            for i, (o_ap, i_ap) in enumerate(per_core[c]):
                key = slots[i][0]
                self.nc.sync.dma_start(out=o_ap, in_=i_ap).then_inc(self.sems[key], 16)
        for i in range(k):
            slots[i][1] += 16
        for i in range(k):
            self._wait(e, (slots[i][0], slots[i][1]))
        for i in range(k):
            self._commit((slots[i][0], slots[i][1]), reads, writes)
        self.n_inst += k


    def barrier(self):
        toks = [("prog_" + e.name, e.count) for e in self.E.values() if e.count > 0]
        for q in self.dq:
            toks += [(key, cnt) for key, cnt in self.dq[q][0] if cnt > 0]
        if self.cc_count:
            toks.append(("cc", self.cc_count))
        for e in self.E.values():
            for tok in toks:
                if tok[0] == "prog_" + e.name:
                    continue
                self._wait(e, tok)

    def cast_eng(self):
        self.rr += 1
        return ("dve", "pool", "act")[self.rr % 3]

    def copy(self, eng, out, in_, reads, writes):
        if eng == "act":
            self.op("act", lambda e: e.activation(out=out, in_=in_, func=AF.Copy), reads, writes)
        else:
            self.op(eng, lambda e: e.tensor_copy(out=out, in_=in_), reads, writes)

    def finish(self, bufs):
        e = self.E["sp"]
        for b in bufs:
            if b.last_w is not None:
                self._wait(e, b.last_w)
        for q in self.dq:
            for key, cnt in self.dq[q][0]:
                if cnt > 0:
                    self._wait(e, (key, cnt))


class WeightPlan:
    def __init__(self):
        self.mats = {}
        self.np_ = 0

    def add(self, name, K, N):
        nkg, nng = -(-K // 512), -(-N // 512)
        self.mats[name] = (self.np_, K, N, nkg, nng)
        self.np_ += nkg * nng

    @property
    def n_units(self):
        return -(-self.np_ // 4)

    def piece_index(self, name, ng, kg):
        first, K, N, nkg, nng = self.mats[name]
        return first + ng * nkg + kg

    def host_pack(self, arrays):
        NP = self.n_units * 4
        allp = np.zeros((NP, 128, 4, 512), np.float32)
        for name, (first, K, N, nkg, nng) in self.mats.items():
            W = arrays[name]
            Kp, Np_ = nkg * 512, nng * 512
            if (Kp, Np_) != W.shape:
                Wp = np.zeros((Kp, Np_), np.float32)
                Wp[:K, :N] = W
            else:
                Wp = W
            v = Wp.reshape(nkg, 4, 128, nng, 512).transpose(3, 0, 2, 1, 4)
            allp[first:first + nkg * nng] = v.reshape(nkg * nng, 128, 4, 512)
        allp = allp.reshape(self.n_units, 4, 128, 2, 1024)
        out = []
        for r in range(NCORES):
            out.append(np.ascontiguousarray(allp[:, r // 2, :, r % 2, :]).reshape(self.n_units * 128, 1024))
        return out


class XUnit:
    def __init__(self, fw, name, rows, cols, dtype):
        self.src = fw.dram(name + "_s", [rows, cols], dtype)
        self.mid = fw.dram(name + "_m", [4 * rows, cols], dtype)
        self.full = fw.dram(name + "_f", [8 * rows, cols], dtype)
        self.rows, self.cols = rows, cols

    def gather(self, fw):
        fw.allgather8(self.src, self.mid, self.full)


def gather_weights(fw, plan, wsh_d):
    stage = [(fw.dram(f"wa{i}", [128, 1024], F32), fw.dram(f"wm{i}", [512, 1024], F32)) for i in range(4)]
    units = []
    for u in range(plan.n_units):
        a, m = stage[u % 4]
        f = fw.dram(f"wu{u}", [1024, 1024], F32)
        fw.dma("sp", a.ap()[:, :], wsh_d.ap()[u * 128:(u + 1) * 128, :], reads=[wsh_d], writes=[a])
        fw.allgather8(a, m, f)
        units.append(f)
    return units


class PCtx:
    def __init__(self, fw, st, D, F, TT, plan, wunits, final=False):
        self.fw, self.D, self.F, self.TT = fw, D, F, TT
        self.plan, self.wunits = plan, wunits
        self.KC = D // 128
        self.FC = -(-F // 128)
        self.ident = fw.sbuf(st, "ident_sb", [128, 128], F32)
        self.NW = 2
        self.wst = [fw.sbuf(st, f"wst{i}", [128, 4, 512], F32) for i in range(self.NW)]
        self.wbf = [fw.sbuf(st, f"wbf{i}", [128, 4, 512], BF16) for i in range(self.NW)]
        self.wi = 0
        self.ps = [fw.psum(st, f"ps{i}", [128, 512], F32) for i in range(8)]
        self.aT = fw.sbuf(st, "aT", [128, self.KC, TT], BF16)
        self.act = fw.sbuf(st, "actT", [128, self.FC, TT], BF16)
        self.hrow = fw.sbuf(st, "hrow", [128, D], F32)
        self.gT = fw.sbuf(st, "gT", [128, self.KC], F32)
        self.gb = fw.sbuf(st, "gb", [128, D], F32) if final else None
        self.small = [fw.sbuf(st, f"sm{i}", [128, 16], F32) for i in range(2)]
        self.ev = [fw.sbuf(st, f"ev{i}", [128, 512], F32) for i in range(4)]
        self.evb = [fw.sbuf(st, f"evb{i}", [128, 512], BF16) for i in range(4)]
        self.evi = 0
        self.hi = 0
        self.gsil = fw.sbuf(st, "gsil", [128, 4, TT], BF16)
        self.bank_flip = 0

    def next_ev(self):
        self.evi += 1
        return self.ev[self.evi % 4]

    def next_evb(self):
        self.evi += 1
        return self.evb[self.evi % 4]

    def bank_set(self):
        self.bank_flip ^= 1
        return self.ps[0:4] if self.bank_flip else self.ps[4:8]


def load_gainT(fw, pc, g_d, row):
    src = g_d.ap()[row, :].rearrange("(k p) -> p k", p=128)
    with fw.nc.allow_non_contiguous_dma(reason="tiny gain vector"):
        fw.dma("sp", pc.gT[:, :], src, reads=[g_d], writes=[pc.gT])


def norm_tile(fw, pc, h_d, tok0, ntok, out_d=None, g_d=None):
    D, KC = pc.D, pc.KC
    nparts = D // 512
    for s in range(ntok // 128):
        pc.hi += 1
        ht = pc.hrow
        sm = pc.small[pc.hi % 2]
        r0 = tok0 + s * 128
        fw.dma("act", ht[:, :], h_d.ap()[r0:r0 + 128, :], reads=[h_d], writes=[ht])
        ev = pc.next_ev()
        for c in range(nparts):
            fw.op("act", lambda e, c=c: e.activation(out=ev[:, :], in_=ht[:, c * 512:(c + 1) * 512], func=AF.Square,
                                                     accum_out=sm[:, c:c + 1]), reads=[ht], writes=[ev, sm])
        fw.op("dve", lambda e: e.tensor_reduce(out=sm[:, 15:16], in_=sm[:, 0:nparts], axis=AX.X, op=ALU.add),
              reads=[sm], writes=[sm])
        fw.op("dve", lambda e: e.tensor_scalar(out=sm[:, 15:16], in0=sm[:, 15:16], scalar1=1.0 / D, scalar2=RMS_EPS,
                                               op0=ALU.mult, op1=ALU.add), reads=[sm], writes=[sm])
        fw.op("act", lambda e: e.activation(out=sm[:, 15:16], in_=sm[:, 15:16], func=AF.Ln), reads=[sm], writes=[sm])
        fw.op("act", lambda e: e.activation(out=sm[:, 15:16], in_=sm[:, 15:16], func=AF.Exp, scale=-0.5),
              reads=[sm], writes=[sm])
        if out_d is not None:
            fw.op("dve", lambda e: e.scalar_tensor_tensor(out=ht[:, :], in0=ht[:, :], scalar=sm[:, 15:16], in1=pc.gb[:, :],
                                                          op0=ALU.mult, op1=ALU.mult), reads=[ht, sm, pc.gb], writes=[ht])
            fw.dma("sp", out_d.ap()[r0:r0 + 128, :], ht[:, :], reads=[ht], writes=[out_d])
            continue
        fw.op("dve", lambda e: e.tensor_scalar(out=ht[:, :], in0=ht[:, :], scalar1=sm[:, 15:16], scalar2=None,
                                               op0=ALU.mult), reads=[ht, sm], writes=[ht])
        for k4 in range(0, KC, 4):
            bank = pc.ps[(k4 // 4) % 8]
            for j in range(4):
                fw.op("pe", lambda e, j=j: e.matmul(bank[:, j * 128:(j + 1) * 128],
                                                    lhsT=ht[:, (k4 + j) * 128:(k4 + j + 1) * 128],
                                                    rhs=pc.ident[:, :], is_transpose=True, start=True, stop=True),
                      reads=[ht, pc.ident], writes=[bank])
            for j in range(4):
                eng = "dve" if j % 2 else "pool"
                if eng == "pool":
                    eng = "dve"
                fw.op(eng, lambda e, j=j: e.tensor_scalar(out=pc.aT[:, k4 + j, s * 128:(s + 1) * 128],
                                                          in0=bank[:, j * 128:(j + 1) * 128],
                                                          scalar1=pc.gT[:, k4 + j:k4 + j + 1], scalar2=None, op0=ALU.mult),
                      reads=[bank, pc.gT], writes=[pc.aT])


def stream_weights(fw, pc, wname, ng, k_chunks, ncols, body):
    nkg = -(-k_chunks // 4)
    for kg in range(nkg):
        nk = min(4, k_chunks - kg * 4)
        pi = pc.plan.piece_index(wname, ng, kg)
        unit = pc.wunits[pi // 4]
        q = pi % 4
        pc.wi += 1
        wst, wbf = pc.wst[pc.wi % pc.NW], pc.wbf[pc.wi % pc.NW]
        src = unit.ap()[q * 256:(q + 1) * 256, :].rearrange("(h p) c -> p h c", p=128)
        fw.dma("sp", wst[:, :, :].rearrange("p j c -> p (j c)").rearrange("p (h c) -> p h c", h=2), src,
               reads=[unit], writes=[wst])
        fw.copy(fw.cast_eng(), wbf[:, 0:nk, 0:ncols], wst[:, 0:nk, 0:ncols], [wst], [wbf])
        for j in range(nk):
            body(kg * 4 + j, wbf, j)


def linear_b(fw, pc, wname, ng, xT, k_chunks, ncols, TT, banks, evac):
    nch = -(-ncols // 128)

    def body(kc, wbf, j):
        for c in range(nch):
            cw = min(128, ncols - c * 128)
            fw.op("pe", lambda e, c=c, cw=cw: e.matmul(banks[c][0:cw, 0:TT], lhsT=wbf[:, j, c * 128:c * 128 + cw],
                                                       rhs=xT[:, kc, 0:TT], start=(kc == 0), stop=(kc == k_chunks - 1)),
                  reads=[wbf, xT], writes=[banks[c]])
    stream_weights(fw, pc, wname, ng, k_chunks, ncols, body)
    for c in range(nch):
        evac(c, min(128, ncols - c * 128), banks[c])


def linear_a(fw, pc, wname, ng, xT, k_chunks, ncols, TT, banks, evac):
    ns = TT // 128

    def body(kc, wbf, j):
        for s in range(ns):
            fw.op("pe", lambda e, s=s: e.matmul(banks[s][:, 0:ncols], lhsT=xT[:, kc, s * 128:(s + 1) * 128],
                                                rhs=wbf[:, j, 0:ncols], start=(kc == 0), stop=(kc == k_chunks - 1)),
                  reads=[wbf, xT], writes=[banks[s]])
    stream_weights(fw, pc, wname, ng, k_chunks, ncols, body)
    for s in range(ns):
        evac(s, banks[s])


def residual_linear_a(fw, pc, wname, xT, k_chunks, h_in, h_out, tok0, TT):
    for ng in range(pc.D // 512):
        def ev(s, bank, ng=ng):
            t = pc.next_ev()
            r0 = tok0 + s * 128
            fw.dma("act", t[:, :], h_in.ap()[r0:r0 + 128, ng * 512:(ng + 1) * 512], reads=[h_in], writes=[t])
            fw.op("dve", lambda e: e.tensor_tensor(out=t[:, :], in0=bank[:, :], in1=t[:, :], op=ALU.add),
                  reads=[bank, t], writes=[t])
            fw.dma("act", h_out.ap()[r0:r0 + 128, ng * 512:(ng + 1) * 512], t[:, :], reads=[t], writes=[h_out])
        linear_a(fw, pc, wname, ng, xT, k_chunks, 512, TT, pc.bank_set(), ev)


def ffn_phase(fw, pc, L, h_in, h_out, tok0, TT):
    F = pc.F
    gs = pc.gsil
    for ng in range(-(-F // 512)):
        ncols = min(512, F - ng * 512)

        def ev_gate(c, cw, bank):
            fw.op("act", lambda e: e.activation(out=gs[0:cw, c, 0:TT], in_=bank[0:cw, 0:TT], func=AF.Silu),
                  reads=[bank], writes=[gs])
        linear_b(fw, pc, f"wg{L}", ng, pc.aT, pc.KC, ncols, TT, pc.bank_set(), ev_gate)

        def ev_up(c, cw, bank, ng=ng):
            fc = ng * 4 + c
            fw.op("dve", lambda e: e.tensor_tensor(out=pc.act[0:cw, fc, 0:TT], in0=bank[0:cw, 0:TT], in1=gs[0:cw, c, 0:TT],
                                                   op=ALU.mult), reads=[bank, gs], writes=[pc.act])
        linear_b(fw, pc, f"wu{L}", ng, pc.aT, pc.KC, ncols, TT, pc.bank_set(), ev_up)
    residual_linear_a(fw, pc, f"wd{L}", pc.act, pc.FC, h_in, h_out, tok0, TT)


NEG = -1e30


class ACtx:
    def __init__(self, fw, st, nkmax, nearmax, identb_d):
        self.fw = fw
        self.ps_s = [fw.psum(st, f"aps{i}", [128, 512], F32) for i in range(2)]
        self.ps_o = fw.psum(st, "apo", [128, 128], F32)
        self.ps_t = [fw.psum(st, f"apt{i}", [128, 512], BF16) for i in range(2)]
        self.P = fw.sbuf(st, "aP", [128, nkmax], BF16)
        self.PT = [fw.sbuf(st, f"aPT{i}", [128, 512], BF16) for i in range(2)]
        self.tmp = fw.sbuf(st, "atmp", [128, nearmax], F32)
        self.mx = fw.sbuf(st, "amx", [128, 80], F32)
        self.rs = fw.sbuf(st, "ars", [128, 80], F32)
        self.sc = fw.sbuf(st, "asc", [128, 8], F32)
        self.identb = fw.sbuf(st, "aidb", [128, 128], BF16)
        fw.dma("sp", self.identb[:, :], identb_d.ap()[:, :], reads=[identb_d], writes=[self.identb])
        self.si = 0
        self.ti = 0


def attn_rowblock(fw, ac, qT, chunks, vfn, out, out_bufs, first, gate=None, c_far=None, rowvalid=None,
                  post=None, no_pv=False):
    qb_, q_ap = qT
    sc, mx, rs, P, tmp = ac.sc, ac.mx, ac.rs, ac.P, ac.tmp
    far_ids = [i for i, c in enumerate(chunks) if c["near"] is None]
    near_ids = [i for i, c in enumerate(chunks) if c["near"] is not None]
    toff = {}
    o = 0
    for i in near_ids:
        toff[i] = o
        o += chunks[i]["kn"]

    def qk(ch, bank):
        kb_, k_ap = ch["kT"]
        kn = ch["kn"]
        fw.op("pe", lambda e: e.matmul(bank[:, 0:kn], lhsT=q_ap, rhs=k_ap, start=True, stop=(ch["aug"] is None)),
              reads=list(qb_) + list(kb_), writes=[bank])
        if ch["aug"] is not None:
            ab_, l_ap, r_ap = ch["aug"]
            fw.op("pe", lambda e: e.matmul(bank[:, 0:kn], lhsT=l_ap, rhs=r_ap, start=False, stop=True),
                  reads=list(ab_), writes=[bank])

    for i, ch in enumerate(chunks):
        ac.si += 1
        bank = ac.ps_s[ac.si % 2]
        kn = ch["kn"]
        qk(ch, bank)
        if ch["near"] is not None:
            nb_, n_ap = ch["near"]
            t_ap = tmp[:, toff[i]:toff[i] + kn]
            fw.op("dve", lambda e: e.tensor_tensor(out=t_ap, in0=bank[:, 0:kn], in1=n_ap, op=ALU.add),
                  reads=[bank] + list(nb_), writes=[tmp])
            fw.op("dve", lambda e: e.tensor_reduce(out=mx[:, i:i + 1], in_=t_ap, axis=AX.X, op=ALU.max),
                  reads=[tmp], writes=[mx])
        else:
            fw.op("dve", lambda e: e.tensor_reduce(out=mx[:, i:i + 1], in_=bank[:, 0:kn], axis=AX.X, op=ALU.max),
                  reads=[bank], writes=[mx])
    n = len(chunks)
    nf = len(far_ids)
    if nf:
        fw.op("dve", lambda e: e.tensor_reduce(out=sc[:, 0:1], in_=mx[:, 0:nf], axis=AX.X, op=ALU.max), reads=[mx], writes=[sc])
        if c_far is not None:
            cb_, c_ap = c_far
            fw.op("dve", lambda e: e.tensor_tensor(out=sc[:, 0:1], in0=sc[:, 0:1], in1=c_ap, op=ALU.add),
                  reads=[sc] + list(cb_), writes=[sc])
    if n > nf:
        fw.op("dve", lambda e: e.tensor_reduce(out=sc[:, 1:2], in_=mx[:, nf:n], axis=AX.X, op=ALU.max), reads=[mx], writes=[sc])
    if nf and n > nf:
        fw.op("dve", lambda e: e.tensor_tensor(out=sc[:, 2:3], in0=sc[:, 0:1], in1=sc[:, 1:2], op=ALU.max), reads=[sc], writes=[sc])
    elif nf:
        fw.op("dve", lambda e: e.tensor_copy(out=sc[:, 2:3], in_=sc[:, 0:1]), reads=[sc], writes=[sc])
    else:
        fw.op("dve", lambda e: e.tensor_copy(out=sc[:, 2:3], in_=sc[:, 1:2]), reads=[sc], writes=[sc])
    fw.op("dve", lambda e: e.tensor_scalar(out=sc[:, 3:4], in0=sc[:, 2:3], scalar1=-1.0, scalar2=None, op0=ALU.mult),
          reads=[sc], writes=[sc])
    if c_far is not None:
        cb_, c_ap = c_far
        fw.op("dve", lambda e: e.tensor_tensor(out=sc[:, 4:5], in0=sc[:, 3:4], in1=c_ap, op=ALU.add),
              reads=[sc] + list(cb_), writes=[sc])
    for i, ch in enumerate(chunks):
        kn, k0 = ch["kn"], ch["k0"]
        if ch["near"] is not None:
            t_ap = tmp[:, toff[i]:toff[i] + kn]
            fw.op("act", lambda e: e.activation(out=P[:, k0:k0 + kn], in_=t_ap, func=AF.Exp, bias=sc[:, 3:4],
                                                accum_out=rs[:, i:i + 1]), reads=[tmp, sc], writes=[P, rs])
        else:
            ac.si += 1
            bank = ac.ps_s[ac.si % 2]
            qk(ch, bank)
            b_ap = sc[:, 4:5] if c_far is not None else sc[:, 3:4]
            fw.op("act", lambda e: e.activation(out=P[:, k0:k0 + kn], in_=bank[:, 0:kn], func=AF.Exp, bias=b_ap,
                                                accum_out=rs[:, i:i + 1]), reads=[bank, sc], writes=[P, rs])
    fw.op("dve", lambda e: e.tensor_reduce(out=sc[:, 5:6], in_=rs[:, 0:n], axis=AX.X, op=ALU.add), reads=[rs], writes=[sc])
    fw.op("dve", lambda e: e.reciprocal(out=sc[:, 6:7], in_=sc[:, 5:6]), reads=[sc], writes=[sc])
    if rowvalid is not None:
        rb_, r_ap = rowvalid
        fw.op("dve", lambda e: e.tensor_tensor(out=sc[:, 6:7], in0=sc[:, 6:7], in1=r_ap, op=ALU.mult),
              reads=[sc] + list(rb_), writes=[sc])
    if post is not None:
        post(sc[:, 6:7])
    if no_pv:
        return
    if gate is not None:
        gb_, g_ap = gate
        fw.op("dve", lambda e: e.tensor_tensor(out=sc[:, 6:7], in0=sc[:, 6:7], in1=g_ap, op=ALU.mult),
              reads=[sc] + list(gb_), writes=[sc])
    nk = chunks[-1]["k0"] + chunks[-1]["kn"]
    nblk = -(-nk // 128)
    if nk % 128:
        fw.op("pool", lambda e: e.memset(P[:, nk:nblk * 128], 0.0), writes=[P])
    for b4 in range(0, nblk, 4):
        nb = min(4, nblk - b4)
        ac.ti += 1
        bank_t, PT = ac.ps_t[ac.ti % 2], ac.PT[ac.ti % 2]
        for j in range(nb):
            fw.op("pe", lambda e, j=j: e.matmul(bank_t[:, j * 128:(j + 1) * 128], lhsT=P[:, (b4 + j) * 128:(b4 + j + 1) * 128],
                                                rhs=ac.identb[:, :], is_transpose=True, start=True, stop=True), reads=[P, ac.identb], writes=[bank_t])
        eng = "act" if ac.ti % 3 == 0 else "dve"
        fw.copy(eng, PT[:, 0:nb * 128], bank_t[:, 0:nb * 128], [bank_t], [PT])
        for j in range(nb):
            vb_, v_ap = vfn(b4 + j)
            fw.op("pe", lambda e, j=j, v_ap=v_ap: e.matmul(ac.ps_o[:, :], lhsT=PT[:, j * 128:(j + 1) * 128], rhs=v_ap,
                                                            start=(b4 + j == 0), stop=(b4 + j == nblk - 1)),
                  reads=[PT] + list(vb_), writes=[ac.ps_o])
    if first:
        fw.op("dve", lambda e: e.tensor_scalar(out=out, in0=ac.ps_o[:, :], scalar1=sc[:, 6:7], scalar2=None, op0=ALU.mult),
              reads=[ac.ps_o, sc], writes=list(out_bufs))
    else:
        fw.op("dve", lambda e: e.scalar_tensor_tensor(out=out, in0=ac.ps_o[:, :], scalar=sc[:, 6:7], in1=out,
                                                      op0=ALU.mult, op1=ALU.add), reads=[ac.ps_o, sc] + list(out_bufs),
              writes=list(out_bufs))


class Cfg:
    def __init__(self, D, T, F, depth=4):
        self.D, self.T, self.F, self.depth = D, T, F, depth
        self.H = D // 128
        self.HPC = self.H // NCORES
        self.TC = T // NCORES
        self.TT = min(512, self.TC)
        self.KC = D // 128
        self.G = 4
        self.HG = self.H // 4
        self.NSA_IN = D + 6 * 512 + 3 * self.H
        self.FOX_IN = 3 * D + self.H


class FoxX:
    def __init__(self, fw, cfg):
        H, TC, T, HPC = cfg.H, cfg.TC, cfg.T, cfg.HPC
        self.xq = [XUnit(fw, f"fxq{h}", 128, TC, BF16) for h in range(H)]
        self.xk = [XUnit(fw, f"fxk{h}", 128, TC, BF16) for h in range(H)]
        self.xv = [XUnit(fw, f"fxv{h}", TC, 128, BF16) for h in range(H)]
        self.xlf = XUnit(fw, "fxlf", H, TC, F32)
        self.lq = [fw.dram(f"flq{j}", [1024, TC], BF16) for j in range(HPC)]
        self.lk = [fw.dram(f"flk{j}", [1024, TC], BF16) for j in range(HPC)]
        self.lv = [fw.dram(f"flv{j}", [T, 128], BF16) for j in range(HPC)]
        self.llf = fw.dram("fllf", [HPC, 8 * TC], F32)
        self.augq = fw.dram("faugq", [HPC, 6, T], BF16)
        self.augk = fw.dram("faugk", [HPC, 6, T], BF16)


class OutX:
    def __init__(self, fw, cfg, name, nslot, cols):
        self.u = [[XUnit(fw, f"{name}{j}_{s}", 128, cols, BF16) for s in range(NCORES)] for j in range(nslot)]


def p1_fox(fw, pc, cfg, fx, L, h_d, consts):
    D, TC, TT, KC, H = cfg.D, cfg.TC, cfg.TT, cfg.KC, cfg.H
    nq = D // 512
    wname = f"fin{L}"
    negb = consts["negb"]
    with fw.nc.allow_non_contiguous_dma(reason="tiny bias vector"):
        fw.dma("sp", negb[0:H, 0:1], consts["b_f"].ap()[L, :].rearrange("(h o) -> h o", o=1), reads=[consts["b_f"]], writes=[negb])
    fw.op("dve", lambda e: e.tensor_scalar(out=negb[0:H, 0:1], in0=negb[0:H, 0:1], scalar1=-1.0, scalar2=None, op0=ALU.mult),
          reads=[negb], writes=[negb])
    for t0 in range(0, TC, TT):
        norm_tile(fw, pc, h_d, t0, TT)
        for ng in range(3 * nq):
            kind = ng // nq
            if kind < 2:
                def ev(c, cw, bank, ng=ng, kind=kind):
                    h = (ng % nq) * 4 + c
                    t = pc.next_evb()
                    if kind == 0:
                        fw.op("act", lambda e: e.activation(out=t[:, 0:TT], in_=bank[:, 0:TT], func=AF.Copy, scale=128 ** -0.5),
                              reads=[bank], writes=[t])
                    else:
                        fw.copy("dve", t[:, 0:TT], bank[:, 0:TT], [bank], [t])
                    unit = fx.xq[h] if kind == 0 else fx.xk[h]
                    fw.dma("act", unit.src.ap()[:, t0:t0 + TT], t[:, 0:TT], reads=[t], writes=[unit.src])
                linear_b(fw, pc, wname, ng, pc.aT, KC, 512, TT, pc.bank_set(), ev)
            else:
                def ev(s_, bank, ng=ng):
                    t = pc.next_evb()
                    fw.copy("act" if s_ % 2 else "dve", t[:, :], bank[:, :], [bank], [t])
                    for c in range(4):
                        h = (ng % nq) * 4 + c
                        fw.dma("act", fx.xv[h].src.ap()[t0 + s_ * 128:t0 + (s_ + 1) * 128, :], t[:, c * 128:(c + 1) * 128],
                               reads=[t], writes=[fx.xv[h].src])
                linear_a(fw, pc, wname, ng, pc.aT, KC, 512, TT, pc.bank_set(), ev)

        def ev_lf(c, cw, bank):
            t = pc.next_ev()
            fw.op("dve", lambda e: e.tensor_scalar(out=t[0:H, 0:TT], in0=bank[0:H, 0:TT], scalar1=-1.0, scalar2=negb[0:H, 0:1],
                                                   op0=ALU.mult, op1=ALU.add), reads=[bank, negb], writes=[t])
            fw.op("act", lambda e: e.activation(out=t[0:H, 0:TT], in_=t[0:H, 0:TT], func=AF.Exp), reads=[t], writes=[t])
            fw.op("dve", lambda e: e.tensor_scalar(out=t[0:H, 0:TT], in0=t[0:H, 0:TT], scalar1=1.0, scalar2=None, op0=ALU.add),
                  reads=[t], writes=[t])
            fw.op("act", lambda e: e.activation(out=t[0:H, 0:TT], in_=t[0:H, 0:TT], func=AF.Ln), reads=[t], writes=[t])
            fw.dma("act", fx.xlf.src.ap()[:, t0:t0 + TT], t[0:H, 0:TT], reads=[t], writes=[fx.xlf.src])
        linear_b(fw, pc, wname, 3 * nq, pc.aT, KC, H, TT, pc.bank_set(), ev_lf)


def fox_exchange(fw, cfg, fx):
    H, HPC, TC = cfg.H, cfg.HPC, cfg.TC
    for h in range(H):
        fx.xq[h].gather(fw)
        fx.xk[h].gather(fw)
        fx.xv[h].gather(fw)
    fx.xlf.gather(fw)
    per_core = []
    reads, writes = [fx.xlf.full], [fx.llf]
    for c in range(NCORES):
        lst = []
        for j in range(HPC):
            h = c * HPC + j
            lst.append((fx.lq[j].ap()[:, :], fx.xq[h].full.ap()[:, :]))
            lst.append((fx.lk[j].ap()[:, :], fx.xk[h].full.ap()[:, :]))
            lst.append((fx.lv[j].ap()[:, :], fx.xv[h].full.ap()[:, :]))
            lst.append((fx.llf.ap()[j:j + 1, :].rearrange("o (r t) -> o r t", r=8),
                        fx.xlf.full.ap().rearrange("(r h) t -> h r t", h=H)[h:h + 1, :, :]))
        per_core.append(lst)
    for h in range(H):
        reads += [fx.xq[h].full, fx.xk[h].full, fx.xv[h].full]
    for j in range(HPC):
        writes += [fx.lq[j], fx.lk[j], fx.lv[j]]
    fw.switch_dmas(per_core, reads, writes)


def fox_prep(fw, st, cfg, fx):
    HPC, T = cfg.HPC, cfg.T
    SEG = min(2048, T)
    sp_t = fw.sbuf(st, "fp_sp", [HPC, SEG], F32)
    cs_t = fw.sbuf(st, "fp_cs", [HPC, SEG], F32)
    ones_t = fw.sbuf(st, "fp_one", [HPC, SEG], F32)
    pb = [fw.sbuf(st, f"fp_pb{i}", [HPC, SEG], BF16) for i in range(3)]
    nb = [fw.sbuf(st, f"fp_nb{i}", [HPC, SEG], BF16) for i in range(3)]
    oneb = fw.sbuf(st, "fp_oneb", [HPC, 3, SEG], BF16)
    carry = fw.sbuf(st, "fp_carry", [HPC, 1], F32)
    fw.op("dve", lambda e: e.memset(ones_t[:, :], 1.0), writes=[ones_t])
    fw.op("dve", lambda e: e.memset(oneb[:, :, :], 1.0), writes=[oneb])
    fw.op("dve", lambda e: e.memset(carry[:, :], 0.0), writes=[carry])
    for s0 in range(0, T, SEG):
        fw.dma("sp", sp_t[:, :], fx.llf.ap()[:, s0:s0 + SEG], reads=[fx.llf], writes=[sp_t])
        fw.op("dve", lambda e: e.tensor_tensor_scan(out=cs_t[:, :], data0=ones_t[:, :], data1=sp_t[:, :], initial=carry[:, 0:1],
                                                    op0=ALU.mult, op1=ALU.add), reads=[ones_t, sp_t, carry], writes=[cs_t])
        fw.op("dve", lambda e: e.tensor_copy(out=carry[:, :], in_=cs_t[:, SEG - 1:SEG]), reads=[cs_t], writes=[carry])
        for i in range(3):
            fw.op("dve", lambda e, i=i: e.tensor_copy(out=pb[i][:, :], in_=cs_t[:, :]), reads=[cs_t], writes=[pb[i]])
            if i < 2:
                fw.op("dve", lambda e, i=i: e.tensor_tensor(out=cs_t[:, :], in0=cs_t[:, :], in1=pb[i][:, :], op=ALU.subtract),
                      reads=[cs_t, pb[i]], writes=[cs_t])
            fw.op("pool", lambda e, i=i: e.tensor_scalar(out=nb[i][:, :], in0=pb[i][:, :], scalar1=-1.0, scalar2=None, op0=ALU.mult),
                  reads=[pb[i]], writes=[nb[i]])
            fw.dma("sp", fx.augk.ap()[:, 3 + i, s0:s0 + SEG], pb[i][:, :], reads=[pb[i]], writes=[fx.augk])
            fw.dma("sp", fx.augq.ap()[:, i, s0:s0 + SEG], nb[i][:, :], reads=[nb[i]], writes=[fx.augq])
        fw.dma("sp", fx.augk.ap()[:, 0:3, s0:s0 + SEG], oneb[:, :, :], reads=[oneb], writes=[fx.augk])
        fw.dma("sp", fx.augq.ap()[:, 3:6, s0:s0 + SEG], oneb[:, :, :], reads=[oneb], writes=[fx.augq])


def fox_attention(fw, cfg, fx, ox, consts):
    T, TC, HPC = cfg.T, cfg.TC, cfg.HPC
    with ExitStack() as st:
        fox_prep(fw, st, cfg, fx)
    fw.barrier()
    with ExitStack() as st:
        ac = ACtx(fw, st, T, 128, consts["identb"])
        KT = fw.sbuf(st, "fKT", [128, T], BF16)
        V = fw.sbuf(st, "fV", [128, T // 128, 128], BF16)
        AK = fw.sbuf(st, "fAK", [6, T], BF16)
        qt = [fw.sbuf(st, f"fq{i}", [128, 128], BF16) for i in range(2)]
        aq = [fw.sbuf(st, f"faq{i}", [6, 128], BF16) for i in range(2)]
        osb = [fw.sbuf(st, f"fo{i}", [128, 128], F32) for i in range(2)]
        otb = [fw.sbuf(st, f"fot{i}", [128, 128], BF16) for i in range(2)]
        tri = fw.sbuf(st, "ftri", [128, 128], F32)
        identf = fw.sbuf(st, "fidf", [128, 128], F32)
        fw.dma("sp", tri[:, :], consts["tri"].ap()[:, :], reads=[consts["tri"]], writes=[tri])
        fw.dma("sp", identf[:, :], consts["identf"].ap()[:, :], reads=[consts["identf"]], writes=[identf])
        for j in range(HPC):
            fw.dma("sp", KT[:, :].rearrange("d (r t) -> d r t", r=8), fx.lk[j].ap().rearrange("(r d) t -> d r t", d=128),
                   reads=[fx.lk[j]], writes=[KT])
            fw.dma("sp", V[:, :, :], fx.lv[j].ap().rearrange("(b p) d -> p b d", p=128), reads=[fx.lv[j]], writes=[V])
            fw.dma("sp", AK[:, :], fx.augk.ap()[j, :, :], reads=[fx.augk], writes=[AK])
            for qb in range(T // 128):
                t0 = qb * 128
                r, tl = t0 // TC, t0 % TC
                q, a, o, ot = qt[qb % 2], aq[qb % 2], osb[qb % 2], otb[qb % 2]
                fw.dma("sp", q[:, :], fx.lq[j].ap()[r * 128:(r + 1) * 128, tl:tl + 128], reads=[fx.lq[j]], writes=[q])
                fw.dma("sp", a[:, :], fx.augq.ap()[j, :, t0:t0 + 128], reads=[fx.augq], writes=[a])
                chunks = []
                for k0 in range(0, t0, 512):
                    kn = min(512, t0 - k0)
                    chunks.append(dict(k0=k0, kn=kn, kT=([KT], KT[:, k0:k0 + kn]), aug=([a, AK], a[:, :], AK[:, k0:k0 + kn]), near=None))
                chunks.append(dict(k0=t0, kn=128, kT=([KT], KT[:, t0:t0 + 128]), aug=([a, AK], a[:, :], AK[:, t0:t0 + 128]),
                                   near=([tri], tri[:, :])))
                attn_rowblock(fw, ac, ([q], q[:, :]), chunks, lambda b: ([V], V[:, b, :]), o[:, :], [o], True)
                bank = ac.ps_s[qb % 2]
                fw.op("pe", lambda e: e.matmul(bank[:, 0:128], lhsT=o[:, :], rhs=identf[:, :], is_transpose=True, start=True, stop=True), reads=[o, identf], writes=[bank])
                fw.copy("act", ot[:, :], bank[:, 0:128], [bank], [ot])
                u = ox.u[j][r]
                fw.dma("act", u.src.ap()[:, tl:tl + 128], ot[:, :], reads=[ot], writes=[u.src])


def p2_phase(fw, pc, cfg, ox, nslot, L, woname, h_in, h_mid, h_out, consts, loader):
    TC, TT, KC = cfg.TC, cfg.TT, cfg.KC
    load_gainT(fw, pc, consts["norm_ffn"], L)
    for t0 in range(0, TC, TT):
        loader(t0)
        residual_linear_a(fw, pc, woname, pc.aT, KC, h_in, h_mid, t0, TT)
        norm_tile(fw, pc, h_mid, t0, TT)
        ffn_phase(fw, pc, L, h_mid, h_out, t0, TT)


def o_localize(fw, cfg, ox, lo):
    HPC = cfg.HPC
    per_core, reads = [], []
    for s_ in range(NCORES):
        lst = []
        for j in range(HPC):
            u = ox.u[j][s_]
            lst.append((lo[j].ap()[:, :], u.full.ap()[:, :]))
            reads.append(u.full)
        per_core.append(lst)
    fw.switch_dmas(per_core, reads, list(lo))


def fox_o_loader(fw, pc, cfg, lo):
    HPC, KC, TT = cfg.HPC, cfg.KC, cfg.TT

    def loader(t0):
        for j in range(HPC):
            fw.dma("sp", pc.aT[:, j:KC:HPC, 0:TT], lo[j].ap().rearrange("(r d) t -> d r t", d=128)[:, :, t0:t0 + TT],
                   reads=[lo[j]], writes=[pc.aT])
    return loader


class NsaX:
    def __init__(self, fw, cfg):
        H, TC, T, HG, G, HPC = cfg.H, cfg.TC, cfg.T, cfg.HG, cfg.G, cfg.HPC
        self.xq = [XUnit(fw, f"nxq{h}", 128, TC, BF16) for h in range(H)]
        self.xkv = {}
        for j6 in range(6):
            for g in range(G):
                if j6 in (3, 5):
                    self.xkv[(j6, g)] = XUnit(fw, f"nxkv{j6}_{g}", TC, 128, BF16)
                else:
                    self.xkv[(j6, g)] = XUnit(fw, f"nxkv{j6}_{g}", 128, TC, BF16)
        self.xg = [XUnit(fw, f"nxg{c}", TC, 3 * HPC, F32) for c in range(NCORES)]
        self.lq = [fw.dram(f"nlq{j}", [1024, TC], BF16) for j in range(2 * HPC)]
        self.lkv = {}
        for j6 in range(6):
            self.lkv[j6] = fw.dram(f"nlkv{j6}", [T, 128] if j6 in (3, 5) else [1024, TC], BF16)
        self.lg = fw.dram("nlg", [T, 3 * HPC], F32)


def p1_nsa(fw, pc, cfg, nx, L, h_d):
    D, TC, TT, KC, H, HG, G = cfg.D, cfg.TC, cfg.TT, cfg.KC, cfg.H, cfg.HG, cfg.G
    nq = D // 512
    wname = f"nin{L}"
    for t0 in range(0, TC, TT):
        norm_tile(fw, pc, h_d, t0, TT)
        for ng in range(nq + 6):
            if ng < nq or (ng - nq) in (0, 1, 2, 4):
                def ev(c, cw, bank, ng=ng):
                    t = pc.next_evb()
                    if ng < nq:
                        unit = nx.xq[ng * 4 + c]
                        fw.op("act", lambda e: e.activation(out=t[:, 0:TT], in_=bank[:, 0:TT], func=AF.Copy, scale=128 ** -0.5),
                              reads=[bank], writes=[t])
                    else:
                        unit = nx.xkv[(ng - nq, c)]
                        fw.copy("dve", t[:, 0:TT], bank[:, 0:TT], [bank], [t])
                    fw.dma("act", unit.src.ap()[:, t0:t0 + TT], t[:, 0:TT], reads=[t], writes=[unit.src])
                linear_b(fw, pc, wname, ng, pc.aT, KC, 512, TT, pc.bank_set(), ev)
            else:
                def ev(s_, bank, ng=ng):
                    t = pc.next_evb()
                    fw.copy("act" if s_ % 2 else "dve", t[:, :], bank[:, :], [bank], [t])
                    for c in range(4):
                        u = nx.xkv[(ng - nq, c)]
                        fw.dma("act", u.src.ap()[t0 + s_ * 128:t0 + (s_ + 1) * 128, :], t[:, c * 128:(c + 1) * 128],
                               reads=[t], writes=[u.src])
                linear_a(fw, pc, wname, ng, pc.aT, KC, 512, TT, pc.bank_set(), ev)

        def ev_g(s_, bank):
            t = pc.next_ev()
            fw.op("act", lambda e: e.activation(out=t[:, 0:3 * H], in_=bank[:, 0:3 * H], func=AF.Sigmoid), reads=[bank], writes=[t])
            for c8 in range(NCORES):
                fw.dma("act", nx.xg[c8].src.ap()[t0 + s_ * 128:t0 + (s_ + 1) * 128, :], t[:, c8 * 3 * cfg.HPC:(c8 + 1) * 3 * cfg.HPC],
                       reads=[t], writes=[nx.xg[c8].src])
        linear_a(fw, pc, wname, nq + 6, pc.aT, KC, 3 * H, TT, pc.bank_set(), ev_g)


def nsa_exchange(fw, cfg, nx):
    H, HG, G, HPC = cfg.H, cfg.HG, cfg.G, cfg.HPC
    reads = []
    for h in range(H):
        nx.xq[h].gather(fw)
        reads.append(nx.xq[h].full)
    for key, u in nx.xkv.items():
        u.gather(fw)
        reads.append(u.full)
    for c8 in range(NCORES):
        nx.xg[c8].gather(fw)
        reads.append(nx.xg[c8].full)
    per_core = []
    for c in range(NCORES):
        g, par = c // 2, c % 2
        lst = []
        for j in range(HPC):
            lst.append((nx.lq[j].ap()[:, :], nx.xq[HPC * c + j].full.ap()[:, :]))
            lst.append((nx.lq[HPC + j].ap()[:, :], nx.xq[HPC * (c ^ 1) + j].full.ap()[:, :]))
        for j6 in range(6):
            lst.append((nx.lkv[j6].ap()[:, :], nx.xkv[(j6, g)].full.ap()[:, :]))
        lst.append((nx.lg.ap()[:, :], nx.xg[c].full.ap()[:, :]))
        per_core.append(lst)
    writes = list(nx.lq) + [nx.lkv[j6] for j6 in range(6)] + [nx.lg]
    fw.switch_dmas(per_core, reads, writes)


def nsa_compress(fw, cfg, nx, consts, Lj, which, KcT, out_kT=None, out_v=None):
    T = cfg.T
    NC = T // 16 - 1
    NCP = -(-(T // 16) // 128) * 128
    src = nx.lkv[which]
    with ExitStack() as st:
        w1s = fw.sbuf(st, "cw1s", [128, 32, 128], F32)
        w1 = fw.sbuf(st, "cw1", [128, 32, 128], BF16)
        w2s = fw.sbuf(st, "cw2s", [128, 128], F32)
        w2 = fw.sbuf(st, "cw2", [128, 128], BF16)
        pos_s = fw.sbuf(st, "cpos_s", [128, 32], F32)
        posb = fw.sbuf(st, "cposb", [128, 32], BF16)
        c1 = fw.sbuf(st, "cc1", [128, 1], F32)
        xs = fw.sbuf(st, "cxs", [128, 512], F32)
        us = fw.sbuf(st, "cus", [128, 512], F32)
        GT = fw.sbuf(st, "cGT", [128, NCP], BF16)
        ps = [fw.psum(st, f"cps{i}", [128, 512], F32) for i in range(2)]
        fw.dma("sp", KcT[:, :].rearrange("d (r t) -> d r t", r=8), src.ap().rearrange("(r d) t -> d r t", d=128), reads=[src], writes=[KcT])
        w1_d, w2_d, pos_d = consts["cmp_w1"], consts["cmp_w2"], consts["cmp_pos"]
        fw.dma("sp", w1s[:, :, :], w1_d.ap()[Lj, which, :, :].rearrange("(j d) e -> d j e", d=128), reads=[w1_d], writes=[w1s])
        fw.copy("dve", w1[:, :, :], w1s[:, :, :], [w1s], [w1])
        fw.dma("sp", w2s[:, :], w2_d.ap()[Lj, which, :, :], reads=[w2_d], writes=[w2s])
        fw.copy("dve", w2[:, :], w2s[:, :], [w2s], [w2])
        with fw.nc.allow_non_contiguous_dma(reason="tiny positional table"):
            fw.dma("sp", pos_s[:, :], pos_d.ap()[Lj, which, :, :].rearrange("j d -> d j"), reads=[pos_d], writes=[pos_s])
        fw.copy("dve", posb[:, :], pos_s[:, :], [pos_s], [posb])
        fw.op("pool", lambda e: e.memset(GT[:, :], 0.0), writes=[GT])
        for j in range(32):
            fw.op("pe", lambda e, j=j: e.matmul(ps[0][:, 0:1], lhsT=w1[:, j, :], rhs=posb[:, j:j + 1], start=(j == 0), stop=(j == 31)),
                  reads=[w1, posb], writes=[ps[0]])
        fw.copy("dve", c1[:, :], ps[0][:, 0:1], [ps[0]], [c1])
        for i, n0 in enumerate(range(0, NC, 512)):
            cnt = min(512, NC - n0)
            bank = ps[(i + 1) % 2]
            for j in range(32):
                a0 = 16 * n0 + j
                fw.op("pe", lambda e, j=j, a0=a0: e.matmul(bank[:, 0:cnt], lhsT=w1[:, j, :], rhs=KcT[:, a0:a0 + 16 * (cnt - 1) + 1:16],
                                                           start=(j == 0), stop=(j == 31)), reads=[w1, KcT], writes=[bank])
            fw.op("act", lambda e: e.activation(out=xs[:, 0:cnt], in_=bank[:, 0:cnt], func=AF.Identity, bias=c1[:, 0:1]),
                  reads=[bank, c1], writes=[xs])
            fw.op("dve", lambda e: e.tensor_tensor(out=us[:, 0:cnt], in0=xs[:, 0:cnt], in1=xs[:, 0:cnt], op=ALU.mult), reads=[xs], writes=[us])
            fw.op("dve", lambda e: e.tensor_scalar(out=us[:, 0:cnt], in0=us[:, 0:cnt], scalar1=0.044715, scalar2=1.0, op0=ALU.mult, op1=ALU.add),
                  reads=[us], writes=[us])
            fw.op("dve", lambda e: e.tensor_tensor(out=us[:, 0:cnt], in0=us[:, 0:cnt], in1=xs[:, 0:cnt], op=ALU.mult), reads=[us, xs], writes=[us])
            fw.op("act", lambda e: e.activation(out=us[:, 0:cnt], in_=us[:, 0:cnt], func=AF.Tanh, scale=0.7978845608028654), reads=[us], writes=[us])
            fw.op("dve", lambda e: e.scalar_tensor_tensor(out=us[:, 0:cnt], in0=us[:, 0:cnt], scalar=1.0, in1=xs[:, 0:cnt], op0=ALU.add, op1=ALU.mult),
                  reads=[us, xs], writes=[us])
            fw.op("act", lambda e: e.activation(out=GT[:, n0:n0 + cnt], in_=us[:, 0:cnt], func=AF.Copy, scale=0.5), reads=[us], writes=[GT])
        if out_kT is not None:
            for i, n0 in enumerate(range(0, NCP, 512)):
                cnt = min(512, NCP - n0)
                bank = ps[i % 2]
                fw.op("pe", lambda e: e.matmul(bank[:, 0:cnt], lhsT=w2[:, :], rhs=GT[:, n0:n0 + cnt], start=True, stop=True), reads=[w2, GT], writes=[bank])
                fw.copy("dve", out_kT[:, n0:n0 + cnt], bank[:, 0:cnt], [bank], [out_kT])
        else:
            for b in range(NCP // 128):
                bank = ps[b % 2]
                fw.op("pe", lambda e, b=b: e.matmul(bank[:, 0:128], lhsT=GT[:, b * 128:(b + 1) * 128], rhs=w2[:, :], start=True, stop=True), reads=[w2, GT], writes=[bank])
                fw.copy("dve", out_v[:, b, :], bank[:, 0:128], [bank], [out_v])
    fw.barrier()


def nsa_attention(fw, cfg, nx, ox, consts, Lj):
    T, TC, HPC = cfg.T, cfg.TC, cfg.HPC
    NC = T // 16 - 1
    NCP = -(-(T // 16) // 128) * 128
    NS = T // 64
    NSP = max(NS, 128)
    nhalf = -(-NS // 128)
    NH2 = 2 * HPC
    with ExitStack() as st0:
        KCT = fw.sbuf(st0, "nKCT", [128, NCP], BF16)
        VC = fw.sbuf(st0, "nVC", [128, NCP // 128, 128], BF16)
        KsT = fw.sbuf(st0, "nKsT", [128, T], BF16)
        nsa_compress(fw, cfg, nx, consts, Lj, 0, KsT, out_kT=KCT)
        nsa_compress(fw, cfg, nx, consts, Lj, 1, KsT, out_v=VC)
        with ExitStack() as st:
            ac = ACtx(fw, st, T, 640, consts["identb"])
            VS = fw.sbuf(st, "nVS", [128, T // 128, 128], BF16)
            WIN = fw.sbuf(st, "nWIN", [128, HPC, 640], F32)
            NEARC = fw.sbuf(st, "nNEARC", [128, NH2, 24], F32)
            C31 = fw.sbuf(st, "nC31", [128, NH2], F32)
            FPAT = fw.sbuf(st, "nFPAT", [128, 4], F32)
            RV = fw.sbuf(st, "nRV", [128, 1], F32)
            WSEL = fw.sbuf(st, "nWSEL", [128, 8192], BF16)
            identf = fw.sbuf(st, "nidf", [128, 128], F32)
            PH = fw.sbuf(st, "nPH", [128, NCP + 16], F32)
            score = fw.sbuf(st, "nscore", [128, NSP], F32)
            score2 = fw.sbuf(st, "nscore2", [128, NSP], F32)
            m8 = fw.sbuf(st, "nm8", [128, 16], F32)
            MB = fw.sbuf(st, "nMB", [128, NSP], BF16)
            MBT = fw.sbuf(st, "nMBT", [128, nhalf, 128], BF16)
            GTl = [fw.sbuf(st, f"nG{i}", [128, 3 * HPC], F32) for i in range(2)]
            qt = [[fw.sbuf(st, f"nq{i}_{hj}", [128, 128], BF16) for hj in range(NH2)] for i in range(2)]
            oacc = [fw.sbuf(st, f"noa{hj}", [128, 128], F32) for hj in range(HPC)]
            otb = [fw.sbuf(st, f"not{i}", [128, 128], BF16) for i in range(2)]
            kw = [fw.sbuf(st, f"nkw{i}", [128, 640], BF16) for i in range(2)]
            vw = [fw.sbuf(st, f"nvw{i}", [128, 5, 128], BF16) for i in range(2)]
            for name, tile_ in (("win", WIN), ("nearc", NEARC), ("c31", C31), ("fpat", FPAT), ("rv", RV), ("wsel", WSEL), ("identf", identf)):
                d = consts[name]
                if len(tile_.t.shape) == 3:
                    fw.dma("sp", tile_[:, :, :], d.ap()[:, :, :], reads=[d], writes=[tile_])
                else:
                    fw.dma("sp", tile_[:, :], d.ap()[:, :], reads=[d], writes=[tile_])
            fw.dma("sp", KsT[:, :].rearrange("d (r t) -> d r t", r=8), nx.lkv[2].ap().rearrange("(r d) t -> d r t", d=128), reads=[nx.lkv[2]], writes=[KsT])
            fw.dma("sp", VS[:, :, :], nx.lkv[3].ap().rearrange("(b p) d -> p b d", p=128), reads=[nx.lkv[3]], writes=[VS])
            fw.op("pool", lambda e: e.memset(score[:, :], -1.0), writes=[score])
            for qb in range(T // 128):
                t0 = qb * 128
                r, tl = t0 // TC, t0 % TC
                G_ = GTl[qb % 2]
                qs = qt[qb % 2]
                fw.dma("sp", G_[:, :], nx.lg.ap()[t0:t0 + 128, :], reads=[nx.lg], writes=[G_])
                for hj in range(NH2):
                    fw.dma("sp", qs[hj][:, :], nx.lq[hj].ap()[r * 128:(r + 1) * 128, tl:tl + 128], reads=[nx.lq[hj]], writes=[qs[hj]])
                wb0 = max(0, qb - 4)
                nwb = qb - wb0 + 1
                kwt, vwt = kw[qb % 2], vw[qb % 2]
                for b in range(nwb):
                    tb = (wb0 + b) * 128
                    rb, tlb = tb // TC, tb % TC
                    fw.dma("act", kwt[:, b * 128:(b + 1) * 128], nx.lkv[4].ap()[rb * 128:(rb + 1) * 128, tlb:tlb + 128], reads=[nx.lkv[4]], writes=[kwt])
                fw.dma("act", vwt[:, 0:nwb, :], nx.lkv[5].ap()[wb0 * 128:(qb + 1) * 128, :].rearrange("(b p) d -> p b d", p=128),
                       reads=[nx.lkv[5]], writes=[vwt])
                nkc = min(8 * qb + 7, NC)
                lo = max(0, 8 * qb - 16)
                fw.op("pool", lambda e: e.memset(PH[:, :], 0.0), writes=[PH])
                for hj in range(NH2):
                    chunks = []
                    for k0 in range(0, lo, 512):
                        kn = min(512, lo - k0)
                        chunks.append(dict(k0=k0, kn=kn, kT=([KCT], KCT[:, k0:k0 + kn]), aug=None, near=None))
                    off = lo - (8 * qb - 16)
                    chunks.append(dict(k0=lo, kn=nkc - lo, kT=([KCT], KCT[:, lo:nkc]), aug=None,
                                       near=([NEARC], NEARC[:, hj, off:off + nkc - lo])))

                    def post(recip, hj=hj):
                        if hj == 0:
                            fw.op("dve", lambda e: e.tensor_scalar(out=PH[:, 0:nkc], in0=ac.P[:, 0:nkc], scalar1=recip, scalar2=None, op0=ALU.mult),
                                  reads=[ac.P, ac.sc], writes=[PH])
                        else:
                            fw.op("dve", lambda e: e.scalar_tensor_tensor(out=PH[:, 0:nkc], in0=ac.P[:, 0:nkc], scalar=recip, in1=PH[:, 0:nkc],
                                                                          op0=ALU.mult, op1=ALU.add), reads=[ac.P, ac.sc, PH], writes=[PH])
                    own = hj < HPC
                    attn_rowblock(fw, ac, ([qs[hj]], qs[hj][:, :]), chunks, lambda b: ([VC], VC[:, b, :]),
                                  oacc[hj][:, :] if own else None, [oacc[hj]] if own else [], True,
                                  gate=([G_], G_[:, hj * 3:hj * 3 + 1]) if own else None,
                                  c_far=([C31], C31[:, hj:hj + 1]), rowvalid=([RV], RV[:, 0:1]) if qb == 0 else None,
                                  post=post, no_pv=not own)
                S_used = min(NS, 2 * qb + 2)
                fw.op("dve", lambda e: e.tensor_reduce(out=score[:, 0:S_used], in_=PH[:, 0:4 * S_used].rearrange("p (s f) -> p s f", f=4),
                                                       axis=AX.X, op=ALU.add), reads=[PH], writes=[score])
                if S_used > 1:
                    fw.op("dve", lambda e: e.tensor_tensor(out=score[:, 1:S_used], in0=score[:, 1:S_used], in1=PH[:, 3:4 * S_used - 4:4], op=ALU.add),
                          reads=[PH, score], writes=[score])
                fw.op("dve", lambda e: e.memset(score[:, 0:1], 1e6), writes=[score])
                if qb == 0:
                    fw.op("dve", lambda e: e.tensor_tensor(out=score[:, 0:2], in0=score[:, 0:2], in1=FPAT[:, 1:3], op=ALU.max), reads=[score, FPAT], writes=[score])
                else:
                    fw.op("dve", lambda e: e.tensor_tensor(out=score[:, 2 * qb - 1:2 * qb + 2], in0=score[:, 2 * qb - 1:2 * qb + 2], in1=FPAT[:, 0:3], op=ALU.max),
                          reads=[score, FPAT], writes=[score])
                fw.op("dve", lambda e: e.max(out=m8[:, 0:8], in_=score[:, 0:NSP]), reads=[score], writes=[m8])
                fw.op("dve", lambda e: e.match_replace(out=score2[:, 0:NSP], in_to_replace=m8[:, 0:8], in_values=score[:, 0:NSP], imm_value=-2.0),
                      reads=[score, m8], writes=[score2])
                fw.op("dve", lambda e: e.max(out=m8[:, 8:16], in_=score2[:, 0:NSP]), reads=[score2], writes=[m8])
                fw.op("dve", lambda e: e.tensor_scalar(out=score2[:, 0:NSP], in0=score[:, 0:NSP], scalar1=m8[:, 15:16], scalar2=None, op0=ALU.is_ge),
                      reads=[score, m8], writes=[score2])
                fw.op("dve", lambda e: e.tensor_scalar(out=MB[:, 0:NSP], in0=score2[:, 0:NSP], scalar1=1e30, scalar2=-1e30, op0=ALU.mult, op1=ALU.add),
                      reads=[score2], writes=[MB])
                for hf in range(nhalf):
                    ac.ti += 1
                    bank_t = ac.ps_t[ac.ti % 2]
                    fw.op("pe", lambda e, hf=hf: e.matmul(bank_t[:, 0:128], lhsT=MB[:, hf * 128:(hf + 1) * 128], rhs=ac.identb[:, :], is_transpose=True,
                                                          start=True, stop=True), reads=[MB, ac.identb], writes=[bank_t])
                    fw.copy("dve", MBT[:, hf, :], bank_t[:, 0:128], [bank_t], [MBT])
                for hj in range(HPC):
                    chunks = []
                    far_end = max(0, t0 - 128)
                    for k0 in range(0, far_end, 512):
                        kn = min(512, far_end - k0)
                        sb = k0 // 64
                        chunks.append(dict(k0=k0, kn=kn, kT=([KsT], KsT[:, k0:k0 + kn]),
                                           aug=([MBT, WSEL], MBT[:, sb // 128, :], WSEL[:, 64 * (sb % 128):64 * (sb % 128) + kn]), near=None))
                    for k0 in range(far_end, t0 + 128, 128):
                        sb = k0 // 64
                        woff = 640 - (t0 + 128 - k0)
                        chunks.append(dict(k0=k0, kn=128, kT=([KsT], KsT[:, k0:k0 + 128]),
                                           aug=([MBT, WSEL], MBT[:, sb // 128, :], WSEL[:, 64 * (sb % 128):64 * (sb % 128) + 128]),
                                           near=([WIN], WIN[:, hj, woff:woff + 128])))
                    attn_rowblock(fw, ac, ([qs[hj]], qs[hj][:, :]), chunks, lambda b: ([VS], VS[:, b, :]), oacc[hj][:, :], [oacc[hj]], False,
                                  gate=([G_], G_[:, hj * 3 + 1:hj * 3 + 2]), c_far=([C31], C31[:, hj:hj + 1]))
                    chunks = []
                    nkw = nwb * 128
                    for k0 in range(0, nkw, 512):
                        kn = min(512, nkw - k0)
                        woff = 640 - nkw + k0
                        chunks.append(dict(k0=k0, kn=kn, kT=([kwt], kwt[:, k0:k0 + kn]), aug=None, near=([WIN], WIN[:, hj, woff:woff + kn])))
                    attn_rowblock(fw, ac, ([qs[hj]], qs[hj][:, :]), chunks, lambda b: ([vwt], vwt[:, b, :]), oacc[hj][:, :], [oacc[hj]], False,
                                  gate=([G_], G_[:, hj * 3 + 2:hj * 3 + 3]))
                    bank = ac.ps_s[hj % 2]
                    ot = otb[hj % 2]
                    fw.op("pe", lambda e, hj=hj: e.matmul(bank[:, 0:128], lhsT=oacc[hj][:, :], rhs=identf[:, :], is_transpose=True, start=True, stop=True),
                          reads=[oacc[hj], identf], writes=[bank])
                    fw.copy("act", ot[:, :], bank[:, 0:128], [bank], [ot])
                    u = ox.u[hj][r]
                    fw.dma("act", u.src.ap()[:, tl:tl + 128], ot[:, :], reads=[ot], writes=[u.src])
        fw.barrier()


def _rel_bucket_np(dist):
    n = np.maximum(dist, 0)
    nf = np.maximum(n, 16).astype(np.float32)
    large = 16 + (np.log(nf / np.float32(16)) / np.float32(np.log(128 / 16)) * np.float32(16)).astype(np.int32)
    large = np.minimum(large, 31)
    return np.where(n < 16, n, large)


def host_nsa_tables(rel_table, H):
    tab = np.asarray(rel_table, np.float32)
    r = np.arange(128)[:, None]
    c = np.arange(640)[None, :]
    rel = r - c + 512
    band = (rel >= 0) & (rel < 512)
    win = np.where(band[None], tab[_rel_bucket_np(rel)].transpose(2, 0, 1), np.float32(NEG)).astype(np.float32)
    npr = np.arange(24)[None, :]
    dist = r + 225 - 16 * npr
    nearc = np.where((dist >= 0)[None], tab[_rel_bucket_np(dist)].transpose(2, 0, 1), np.float32(NEG)).astype(np.float32)
    c31 = tab[31]
    return win, nearc, c31


def host_consts():
    tri = np.where(np.arange(128)[None, :] <= np.arange(128)[:, None], 0.0, NEG).astype(np.float32)
    fpat = np.zeros((128, 4), np.float32)
    fpat[:64, 0] = 1e6
    fpat[:64, 1] = 1e6
    fpat[64:, 1] = 1e6
    fpat[64:, 2] = 1e6
    rv = (np.arange(128) >= 31).astype(np.float32)[:, None]
    wsel = (np.arange(8192)[None, :] // 64 == np.arange(128)[:, None]).astype(np.float32).astype(ml_dtypes.bfloat16)
    return dict(tri=tri, fpat=fpat, rv=rv, wsel=wsel, identf=np.eye(128, dtype=np.float32),
                identb=np.eye(128).astype(ml_dtypes.bfloat16))


def make_plan(cfg, n_layers):
    plan = WeightPlan()
    D, F = cfg.D, cfg.F
    for L in range(n_layers):
        j = L // 2
        if L % 2 == 0:
            plan.add(f"nin{j}", D, cfg.NSA_IN)
            plan.add(f"no{j}", D, D)
        else:
            plan.add(f"fin{j}", D, cfg.FOX_IN)
            plan.add(f"fo{j}", D, D)
        plan.add(f"wg{L}", D, F)
        plan.add(f"wu{L}", D, F)
        plan.add(f"wd{L}", F, D)
    return plan


def build_program(cfg, plan, n_layers, final_norm=True):
    nc = bass.Bass("TRN2", target_bir_lowering=False)
    D, F, T, TC, TT, H, HPC = cfg.D, cfg.F, cfg.T, cfg.TC, cfg.TT, cfg.H, cfg.HPC

    def ext(name, shape, dt=F32):
        return Buf(nc.dram_tensor(name, shape, dt, kind="ExternalInput"), name)
    x = ext("x", [TC, D])
    consts = dict(norm_mix=ext("norm_mix", [4, D]), norm_ffn=ext("norm_ffn", [4, D]), norm_final=ext("norm_final", [1, D]),
                  b_f=ext("b_f", [2, H]), cmp_w1=ext("cmp_w1", [2, 2, 4096, 128]), cmp_w2=ext("cmp_w2", [2, 2, 128, 128]),
                  cmp_pos=ext("cmp_pos", [2, 2, 32, 128]), identf=ext("identf", [128, 128]), identb=ext("identb", [128, 128], BF16),
                  tri=ext("tri", [128, 128]), fpat=ext("fpat", [128, 4]), rv=ext("rv", [128, 1]), wsel=ext("wsel", [128, 8192], BF16),
                  win=ext("win", [128, HPC, 640]), nearc=ext("nearc", [128, 2 * HPC, 24]), c31=ext("c31", [128, 2 * HPC]))
    wsh = ext("wsh", [plan.n_units * 128, 1024])
    out = Buf(nc.dram_tensor("out", [TC, D], F32, kind="ExternalOutput"), "out")
    with ExitStack() as st0:
        fw = FW(nc, st0)
        wunits = gather_weights(fw, plan, wsh)
        fx = FoxX(fw, cfg) if n_layers > 1 else None
        nx = NsaX(fw, cfg)
        ox = OutX(fw, cfg, "oxo", HPC, TC)
        lo = [fw.dram(f"lo{j}", [1024, TC], BF16) for j in range(HPC)]
        hbuf = [fw.dram("h_a", [TC, D], F32), fw.dram("h_b", [TC, D], F32)]
        h_mid = fw.dram("h_mid", [TC, D], F32)
        consts["negb"] = fw.sbuf(st0, "negb", [128, 1], F32)
        h_cur = x
        for L in range(n_layers):
            j = L // 2
            with ExitStack() as st:
                pc = PCtx(fw, st, D, F, TT, plan, wunits)
                fw.dma("sp", pc.ident[:, :], consts["identf"].ap()[:, :], reads=[consts["identf"]], writes=[pc.ident])
                load_gainT(fw, pc, consts["norm_mix"], L)
                if L % 2 == 0:
                    p1_nsa(fw, pc, cfg, nx, j, h_cur)
                else:
                    p1_fox(fw, pc, cfg, fx, j, h_cur, consts)
            fw.barrier()
            if L % 2 == 0:
                nsa_exchange(fw, cfg, nx)
                nsa_attention(fw, cfg, nx, ox, consts, j)
            else:
                fox_exchange(fw, cfg, fx)
                fox_attention(fw, cfg, fx, ox, consts)
            for j_ in range(HPC):
                for s_ in range(NCORES):
                    ox.u[j_][s_].gather(fw)
            o_localize(fw, cfg, ox, lo)
            fw.barrier()
            last = (L == n_layers - 1) and not final_norm
            h_next = out if last else hbuf[L % 2]
            with ExitStack() as st:
                pc = PCtx(fw, st, D, F, TT, plan, wunits)
                fw.dma("sp", pc.ident[:, :], consts["identf"].ap()[:, :], reads=[consts["identf"]], writes=[pc.ident])
                p2_phase(fw, pc, cfg, ox, HPC, L, f"no{j}" if L % 2 == 0 else f"fo{j}", h_cur, h_mid, h_next, consts,
                         fox_o_loader(fw, pc, cfg, lo))
            fw.barrier()
            h_cur = h_next
        if final_norm:
            with ExitStack() as st:
                pc = PCtx(fw, st, D, 128, 128, plan, wunits, final=True)
                fw.dma("sp", pc.gb[:, :], consts["norm_final"].ap()[0:1, :].partition_broadcast(128), reads=[consts["norm_final"]], writes=[pc.gb])
                for t0 in range(0, TC, 128):
                    norm_tile(fw, pc, h_cur, t0, 128, out_d=out)
        fw.finish([out])
    return nc, fw.n_inst


def run_module(inputs, cfg, n_layers, final_norm=True):
    plan = make_plan(cfg, n_layers)
    nc, n_inst = build_program(cfg, plan, n_layers, final_norm)
    D, T, TC, H, HPC = cfg.D, cfg.T, cfg.TC, cfg.H, cfg.HPC
    f32 = np.float32
    W = {}
    for L in range(n_layers):
        j = L // 2
        if L % 2 == 0:
            W[f"nin{j}"] = np.asarray(inputs["nsa_w_in"][j], f32)
            W[f"no{j}"] = np.asarray(inputs["nsa_w_o"][j], f32)
        else:
            W[f"fin{j}"] = np.asarray(inputs["fox_w_in"][j], f32)
            W[f"fo{j}"] = np.asarray(inputs["fox_w_o"][j], f32)
        W[f"wg{L}"] = np.asarray(inputs["ffn_w_gate"][L], f32)
        W[f"wu{L}"] = np.asarray(inputs["ffn_w_up"][L], f32)
        W[f"wd{L}"] = np.asarray(inputs["ffn_w_down"][L], f32)
    wsh = plan.host_pack(W)
    del W
    hc = host_consts()
    win, nearc, c31 = host_nsa_tables(inputs["rel_table"], H)
    x = np.asarray(inputs["x"], f32).reshape(T, D)
    maps = []
    for c in range(NCORES):
        own = [HPC * c + j for j in range(HPC)]
        oth = [HPC * (c ^ 1) + j for j in range(HPC)]
        m = dict(x=np.ascontiguousarray(x[c * TC:(c + 1) * TC]),
                 norm_mix=np.asarray(inputs["norm_mix"], f32), norm_ffn=np.asarray(inputs["norm_ffn"], f32),
                 norm_final=np.asarray(inputs["norm_final"], f32).reshape(1, D), b_f=np.asarray(inputs["fox_b_f"], f32),
                 cmp_w1=np.asarray(inputs["nsa_cmp_w1"], f32), cmp_w2=np.asarray(inputs["nsa_cmp_w2"], f32),
                 cmp_pos=np.asarray(inputs["nsa_cmp_pos"], f32), wsh=wsh[c],
                 win=np.ascontiguousarray(win[own].transpose(1, 0, 2)),
                 nearc=np.ascontiguousarray(nearc[own + oth].transpose(1, 0, 2)),
                 c31=np.ascontiguousarray(np.broadcast_to(c31[own + oth][None, :], (128, 2 * HPC))).astype(f32))
        m.update(hc)
        maps.append(m)
    res = run_bass_kernel_spmd(nc, maps, core_ids=list(range(NCORES)))
    o = np.concatenate([res.results[c]["out"] for c in range(NCORES)], 0)
    return o.reshape(1, T, D).astype(f32)


def kernel(**inputs):
    cfg = Cfg(4096, 16384, 11008)
    return run_module(inputs, cfg, 4)
```

```python
import numpy as np
import ml_dtypes
from contextlib import ExitStack
import concourse.bass as bass
import concourse.mybir as mybir
from concourse.bass_utils import run_bass_kernel_spmd

F32 = mybir.dt.float32
BF16 = mybir.dt.bfloat16
ALU = mybir.AluOpType
AF = mybir.ActivationFunctionType
AX = mybir.AxisListType
NCORES = 8
N_DMA_SEMS = 6
RMS_EPS = 1e-6


class Buf:
    __slots__ = ("t", "last_w", "readers", "name")

    def __init__(self, t, name=""):
        self.t = t
        self.last_w = None
        self.readers = {}
        self.name = name

    def __getitem__(self, idx):
        return self.t[idx]

    def ap(self):
        return self.t.ap()


class _Eng:
    def __init__(self, name, eng, sem, same_sync):
        self.name, self.eng, self.sem, self.same_sync = name, eng, sem, same_sync
        self.count = 0
        self.waited = {}


class FW:
    def __init__(self, nc, stack):
        self.nc, self.stack = nc, stack
        self.sems, self.E = {}, {}
        for name, eng, same in (("pe", nc.tensor, False), ("act", nc.scalar, True),
                                ("dve", nc.vector, True), ("pool", nc.gpsimd, True),
                                ("sp", nc.sync, False)):
            sem = stack.enter_context(nc.semaphore("prog_" + name))
            self.sems["prog_" + name] = sem
            self.E[name] = _Eng(name, eng, sem, same)
        self.dq = {}
        for q in ("sp", "pool", "act"):
            lst = []
            for i in range(N_DMA_SEMS):
                key = f"dma_{q}_{i}"
                self.sems[key] = stack.enter_context(nc.semaphore(key))
                lst.append([key, 0])
            self.dq[q] = [lst, 0]
        self.sems["cc"] = stack.enter_context(nc.semaphore("cc"))
        self.cc_count = 0
        self.n_inst = 0
        self.rr = 0

    def sbuf(self, st, name, shape, dtype):
        self.uid = getattr(self, "uid", 0) + 1
        name = f"{name}_{self.uid}"
        return Buf(st.enter_context(self.nc.sbuf_tensor(name, shape, dtype)), name)

    def psum(self, st, name, shape, dtype=F32):
        self.uid = getattr(self, "uid", 0) + 1
        name = f"{name}_{self.uid}"
        return Buf(st.enter_context(self.nc.psum_tensor(name, shape, dtype)), name)

    def dram(self, name, shape, dtype):
        return Buf(self.nc.dram_tensor(name, shape, dtype, kind="Internal"), name)

    def _wait(self, e, tok):
        key, val = tok
        if key == "prog_" + e.name and not e.same_sync:
            return
        if e.waited.get(key, 0) >= val:
            return
        e.eng.wait_ge(self.sems[key], val)
        e.waited[key] = val

    def _deps(self, e, reads, writes):
        for r in reads:
            if r.last_w is not None:
                self._wait(e, r.last_w)
        for w in writes:
            if w.last_w is not None:
                self._wait(e, w.last_w)
            for tok in list(w.readers.items()):
                self._wait(e, tok)

    def _commit(self, tok, reads, writes):
        for r in reads:
            if r.readers.get(tok[0], 0) < tok[1]:
                r.readers[tok[0]] = tok[1]
        for w in writes:
            w.last_w = tok
            w.readers = {}

    def op(self, eng, fn, reads=(), writes=()):
        e = self.E[eng]
        self._deps(e, reads, writes)
        inst = fn(e.eng)
        e.count += 1
        inst.then_inc(e.sem, 1)
        self._commit(("prog_" + e.name, e.count), reads, writes)
        self.n_inst += 1

    def dma(self, q, out, in_, reads=(), writes=(), **kw):
        e = self.E[q]
        lst, idx = self.dq[q]
        slot = lst[idx % N_DMA_SEMS]
        self.dq[q][1] = idx + 1
        key, cnt = slot
        if cnt > 0:
            self._wait(e, (key, cnt))
        self._deps(e, reads, writes)
        inst = e.eng.dma_start(out=out, in_=in_, **kw)
        slot[1] = cnt + 16
        inst.then_inc(self.sems[key], 16)
        self._commit((key, cnt + 16), reads, writes)
        self.n_inst += 1

    def collective(self, kind, groups, src, dst):
        e = self.E["pool"]
        self._deps(e, [src], [dst])
        inst = self.nc.gpsimd.collective_compute(kind, ALU.bypass, replica_groups=groups,
                                                 ins=[src.ap().opt()], outs=[dst.ap().opt()])
        self.cc_count += 1
        inst.then_inc(self.sems["cc"])
        self._commit(("cc", self.cc_count), [src], [dst])
        self.n_inst += 1

    def allgather8(self, src, mid, dst):
        self.collective("AllGather", [[0, 1, 2, 3], [4, 5, 6, 7]], src, mid)
        self.collective("AllGather", [[0, 4], [1, 5], [2, 6], [3, 7]], mid, dst)


    def switch_dmas(self, per_core, reads, writes):
        e = self.E["sp"]
        k = len(per_core[0])
        slots = []
        lst = self.dq["sp"][0]
        for i in range(k):
            idx = self.dq["sp"][1]
            slot = lst[idx % N_DMA_SEMS]
            self.dq["sp"][1] = idx + 1
            if slot[1] > 0:
                self._wait(e, (slot[0], slot[1]))
            slots.append(slot)
        self._deps(e, reads, writes)
        if getattr(self, "_pid", None) is None:
            self._pid = self.nc.sync.partition_id()
        for c in self.nc.Switch(engines=[self.nc.sync], index=[self._pid], n=NCORES):
            incs = {}
            for i, (o_ap, i_ap) in enumerate(per_core[c]):
                key = slots[i][0]
                self.nc.sync.dma_start(out=o_ap, in_=i_ap).then_inc(self.sems[key], 16)
        for i in range(k):
            slots[i][1] += 16
        for i in range(k):
            self._wait(e, (slots[i][0], slots[i][1]))
        for i in range(k):
            self._commit((slots[i][0], slots[i][1]), reads, writes)
        self.n_inst += k


    def barrier(self):
        toks = [("prog_" + e.name, e.count) for e in self.E.values() if e.count > 0 and e.name != "pool"]
        for q in self.dq:
            if q == "pool":
                continue
            toks += [(key, cnt) for key, cnt in self.dq[q][0] if cnt > 0]
        for e in self.E.values():
            if e.name == "pool":
                continue
            for tok in toks:
                if tok[0] == "prog_" + e.name:
                    continue
                self._wait(e, tok)

    def cast_eng(self):
        self.rr += 1
        return ("dve", "act")[self.rr % 2]

    def copy(self, eng, out, in_, reads, writes):
        if eng == "act":
            self.op("act", lambda e: e.activation(out=out, in_=in_, func=AF.Copy), reads, writes)
        else:
            self.op(eng, lambda e: e.tensor_copy(out=out, in_=in_), reads, writes)

    def finish(self, bufs):
        e = self.E["sp"]
        for b in bufs:
            if b.last_w is not None:
                self._wait(e, b.last_w)
        for q in self.dq:
            for key, cnt in self.dq[q][0]:
                if cnt > 0:
                    self._wait(e, (key, cnt))


class WeightPlan:
    def __init__(self):
        self.mats = {}
        self.np_ = 0

    def add(self, name, K, N):
        nkg, nng = -(-K // 512), -(-N // 512)
        self.mats[name] = (self.np_, K, N, nkg, nng)
        self.np_ += nkg * nng

    @property
    def n_units(self):
        return -(-self.np_ // 4)

    def piece_index(self, name, ng, kg):
        first, K, N, nkg, nng = self.mats[name]
        return first + ng * nkg + kg

    def host_pack(self, arrays):
        NP = self.n_units * 4
        allp = np.zeros((NP, 128, 4, 512), np.float32)
        for name, (first, K, N, nkg, nng) in self.mats.items():
            W = arrays[name]
            Kp, Np_ = nkg * 512, nng * 512
            if (Kp, Np_) != W.shape:
                Wp = np.zeros((Kp, Np_), np.float32)
                Wp[:K, :N] = W
            else:
                Wp = W
            v = Wp.reshape(nkg, 4, 128, nng, 512).transpose(3, 0, 2, 1, 4)
            allp[first:first + nkg * nng] = v.reshape(nkg * nng, 128, 4, 512)
        allp = allp.reshape(self.n_units, 4, 128, 2, 1024)
        out = []
        for r in range(NCORES):
            out.append(np.ascontiguousarray(allp[:, r // 2, :, r % 2, :]).reshape(self.n_units * 128, 1024))
        return out


class XUnit:
    def __init__(self, fw, name, rows, cols, dtype):
        self.src = fw.dram(name + "_s", [rows, cols], dtype)
        self.mid = fw.dram(name + "_m", [4 * rows, cols], dtype)
        self.full = fw.dram(name + "_f", [8 * rows, cols], dtype)
        self.rows, self.cols = rows, cols

    def gather(self, fw):
        fw.allgather8(self.src, self.mid, self.full)


def gather_weights(fw, plan, wsh_d, units=None, stage=None, u0=0, u1=None):
    if stage is None:
        stage = [(fw.dram(f"wa{i}", [128, 1024], F32), fw.dram(f"wm{i}", [512, 1024], F32)) for i in range(4)]
    if units is None:
        units = [fw.dram(f"wu{u}", [1024, 1024], F32) for u in range(plan.n_units)]
    if u1 is None:
        u1 = plan.n_units
    for u in range(u0, u1):
        a, m = stage[u % 4]
        fw.dma("pool", a.ap()[:, :], wsh_d.ap()[u * 128:(u + 1) * 128, :], reads=[wsh_d], writes=[a])
        fw.allgather8(a, m, units[u])
    return units, stage


class PCtx:
    def __init__(self, fw, st, D, F, TT, plan, wunits, final=False):
        self.fw, self.D, self.F, self.TT = fw, D, F, TT
        self.plan, self.wunits = plan, wunits
        self.KC = D // 128
        self.FC = -(-F // 128)
        self.ident = fw.sbuf(st, "ident_sb", [128, 128], F32)
        self.NW = 2
        self.wst = [fw.sbuf(st, f"wst{i}", [128, 4, 512], F32) for i in range(self.NW)]
        self.wbf = [fw.sbuf(st, f"wbf{i}", [128, 4, 512], BF16) for i in range(self.NW)]
        self.wi = 0
        self.ps = [fw.psum(st, f"ps{i}", [128, 512], F32) for i in range(8)]
        self.aT = fw.sbuf(st, "aT", [128, self.KC, TT], BF16)
        self.act = fw.sbuf(st, "actT", [128, self.FC, TT], BF16)
        self.hrow = fw.sbuf(st, "hrow", [128, D], F32)
        self.gT = fw.sbuf(st, "gT", [128, self.KC], F32)
        self.gb = fw.sbuf(st, "gb", [128, D], F32) if final else None
        self.small = [fw.sbuf(st, f"sm{i}", [128, 16], F32) for i in range(2)]
        self.ev = [fw.sbuf(st, f"ev{i}", [128, 512], F32) for i in range(4)]
        self.evb = [fw.sbuf(st, f"evb{i}", [128, 512], BF16) for i in range(4)]
        self.evi = 0
        self.hi = 0
        self.gsil = fw.sbuf(st, "gsil", [128, 4, TT], BF16)
        self.bank_flip = 0

    def next_ev(self):
        self.evi += 1
        return self.ev[self.evi % 4]

    def next_evb(self):
        self.evi += 1
        return self.evb[self.evi % 4]

    def bank_set(self):
        self.bank_flip ^= 1
        return self.ps[0:4] if self.bank_flip else self.ps[4:8]


def load_gainT(fw, pc, g_d, row):
    src = g_d.ap()[row, :].rearrange("(k p) -> p k", p=128)
    with fw.nc.allow_non_contiguous_dma(reason="tiny gain vector"):
        fw.dma("sp", pc.gT[:, :], src, reads=[g_d], writes=[pc.gT])


def norm_tile(fw, pc, h_d, tok0, ntok, out_d=None, g_d=None):
    D, KC = pc.D, pc.KC
    nparts = D // 512
    for s in range(ntok // 128):
        pc.hi += 1
        ht = pc.hrow
        sm = pc.small[pc.hi % 2]
        r0 = tok0 + s * 128
        fw.dma("act", ht[:, :], h_d.ap()[r0:r0 + 128, :], reads=[h_d], writes=[ht])
        ev = pc.next_ev()
        for c in range(nparts):
            fw.op("act", lambda e, c=c: e.activation(out=ev[:, :], in_=ht[:, c * 512:(c + 1) * 512], func=AF.Square,
                                                     accum_out=sm[:, c:c + 1]), reads=[ht], writes=[ev, sm])
        fw.op("dve", lambda e: e.tensor_reduce(out=sm[:, 15:16], in_=sm[:, 0:nparts], axis=AX.X, op=ALU.add),
              reads=[sm], writes=[sm])
        fw.op("dve", lambda e: e.tensor_scalar(out=sm[:, 15:16], in0=sm[:, 15:16], scalar1=1.0 / D, scalar2=RMS_EPS,
                                               op0=ALU.mult, op1=ALU.add), reads=[sm], writes=[sm])
        fw.op("act", lambda e: e.activation(out=sm[:, 15:16], in_=sm[:, 15:16], func=AF.Ln), reads=[sm], writes=[sm])
        fw.op("act", lambda e: e.activation(out=sm[:, 15:16], in_=sm[:, 15:16], func=AF.Exp, scale=-0.5),
              reads=[sm], writes=[sm])
        if out_d is not None:
            fw.op("dve", lambda e: e.scalar_tensor_tensor(out=ht[:, :], in0=ht[:, :], scalar=sm[:, 15:16], in1=pc.gb[:, :],
                                                          op0=ALU.mult, op1=ALU.mult), reads=[ht, sm, pc.gb], writes=[ht])
            fw.dma("sp", out_d.ap()[r0:r0 + 128, :], ht[:, :], reads=[ht], writes=[out_d])
            continue
        fw.op("dve", lambda e: e.tensor_scalar(out=ht[:, :], in0=ht[:, :], scalar1=sm[:, 15:16], scalar2=None,
                                               op0=ALU.mult), reads=[ht, sm], writes=[ht])
        for k4 in range(0, KC, 4):
            bank = pc.ps[(k4 // 4) % 8]
            for j in range(4):
                fw.op("pe", lambda e, j=j: e.matmul(bank[:, j * 128:(j + 1) * 128],
                                                    lhsT=ht[:, (k4 + j) * 128:(k4 + j + 1) * 128],
                                                    rhs=pc.ident[:, :], is_transpose=True, start=True, stop=True),
                      reads=[ht, pc.ident], writes=[bank])
            for j in range(4):
                eng = "dve" if j % 2 else "pool"
                if eng == "pool":
                    eng = "dve"
                fw.op(eng, lambda e, j=j: e.tensor_scalar(out=pc.aT[:, k4 + j, s * 128:(s + 1) * 128],
                                                          in0=bank[:, j * 128:(j + 1) * 128],
                                                          scalar1=pc.gT[:, k4 + j:k4 + j + 1], scalar2=None, op0=ALU.mult),
                      reads=[bank, pc.gT], writes=[pc.aT])


def stream_weights(fw, pc, wname, ng, k_chunks, ncols, body):
    nkg = -(-k_chunks // 4)
    for kg in range(nkg):
        nk = min(4, k_chunks - kg * 4)
        pi = pc.plan.piece_index(wname, ng, kg)
        unit = pc.wunits[pi // 4]
        q = pi % 4
        pc.wi += 1
        wst, wbf = pc.wst[pc.wi % pc.NW], pc.wbf[pc.wi % pc.NW]
        src = unit.ap()[q * 256:(q + 1) * 256, :].rearrange("(h p) c -> p h c", p=128)
        fw.dma("sp", wst[:, :, :].rearrange("p j c -> p (j c)").rearrange("p (h c) -> p h c", h=2), src,
               reads=[unit], writes=[wst])
        fw.copy(fw.cast_eng(), wbf[:, 0:nk, 0:ncols], wst[:, 0:nk, 0:ncols], [wst], [wbf])
        for j in range(nk):
            body(kg * 4 + j, wbf, j)


def linear_b(fw, pc, wname, ng, xT, k_chunks, ncols, TT, banks, evac):
    nch = -(-ncols // 128)

    def body(kc, wbf, j):
        for c in range(nch):
            cw = min(128, ncols - c * 128)
            fw.op("pe", lambda e, c=c, cw=cw: e.matmul(banks[c][0:cw, 0:TT], lhsT=wbf[:, j, c * 128:c * 128 + cw],
                                                       rhs=xT[:, kc, 0:TT], start=(kc == 0), stop=(kc == k_chunks - 1)),
                  reads=[wbf, xT], writes=[banks[c]])
    stream_weights(fw, pc, wname, ng, k_chunks, ncols, body)
    for c in range(nch):
        evac(c, min(128, ncols - c * 128), banks[c])


def linear_a(fw, pc, wname, ng, xT, k_chunks, ncols, TT, banks, evac):
    ns = TT // 128

    def body(kc, wbf, j):
        for s in range(ns):
            fw.op("pe", lambda e, s=s: e.matmul(banks[s][:, 0:ncols], lhsT=xT[:, kc, s * 128:(s + 1) * 128],
                                                rhs=wbf[:, j, 0:ncols], start=(kc == 0), stop=(kc == k_chunks - 1)),
                  reads=[wbf, xT], writes=[banks[s]])
    stream_weights(fw, pc, wname, ng, k_chunks, ncols, body)
    for s in range(ns):
        evac(s, banks[s])


def residual_linear_a(fw, pc, wname, xT, k_chunks, h_in, h_out, tok0, TT):
    for ng in range(pc.D // 512):
        def ev(s, bank, ng=ng):
            t = pc.next_ev()
            r0 = tok0 + s * 128
            fw.dma("act", t[:, :], h_in.ap()[r0:r0 + 128, ng * 512:(ng + 1) * 512], reads=[h_in], writes=[t])
            fw.op("dve", lambda e: e.tensor_tensor(out=t[:, :], in0=bank[:, :], in1=t[:, :], op=ALU.add),
                  reads=[bank, t], writes=[t])
            fw.dma("act", h_out.ap()[r0:r0 + 128, ng * 512:(ng + 1) * 512], t[:, :], reads=[t], writes=[h_out])
        linear_a(fw, pc, wname, ng, xT, k_chunks, 512, TT, pc.bank_set(), ev)


def ffn_phase(fw, pc, L, h_in, h_out, tok0, TT):
    F = pc.F
    gs = pc.gsil
    for ng in range(-(-F // 512)):
        ncols = min(512, F - ng * 512)

        def ev_gate(c, cw, bank):
            fw.op("act", lambda e: e.activation(out=gs[0:cw, c, 0:TT], in_=bank[0:cw, 0:TT], func=AF.Silu),
                  reads=[bank], writes=[gs])
        linear_b(fw, pc, f"wg{L}", ng, pc.aT, pc.KC, ncols, TT, pc.bank_set(), ev_gate)

        def ev_up(c, cw, bank, ng=ng):
            fc = ng * 4 + c
            fw.op("dve", lambda e: e.tensor_tensor(out=pc.act[0:cw, fc, 0:TT], in0=bank[0:cw, 0:TT], in1=gs[0:cw, c, 0:TT],
                                                   op=ALU.mult), reads=[bank, gs], writes=[pc.act])
        linear_b(fw, pc, f"wu{L}", ng, pc.aT, pc.KC, ncols, TT, pc.bank_set(), ev_up)
    residual_linear_a(fw, pc, f"wd{L}", pc.act, pc.FC, h_in, h_out, tok0, TT)


NEG = -1e30


class ACtx:
    def __init__(self, fw, st, nkmax, nearmax, identb_d):
        self.fw = fw
        self.ps_s = [fw.psum(st, f"aps{i}", [128, 512], F32) for i in range(2)]
        self.ps_o = fw.psum(st, "apo", [128, 128], F32)
        self.ps_t = [fw.psum(st, f"apt{i}", [128, 512], BF16) for i in range(2)]
        self.P = fw.sbuf(st, "aP", [128, nkmax], BF16)
        self.PT = [fw.sbuf(st, f"aPT{i}", [128, 512], BF16) for i in range(2)]
        self.tmp = fw.sbuf(st, "atmp", [128, nearmax], F32)
        self.mx = fw.sbuf(st, "amx", [128, 80], F32)
        self.rs = fw.sbuf(st, "ars", [128, 80], F32)
        self.sc = fw.sbuf(st, "asc", [128, 8], F32)
        self.identb = fw.sbuf(st, "aidb", [128, 128], BF16)
        fw.dma("sp", self.identb[:, :], identb_d.ap()[:, :], reads=[identb_d], writes=[self.identb])
        self.si = 0
        self.ti = 0


def attn_rowblock(fw, ac, qT, chunks, vfn, out, out_bufs, first, gate=None, c_far=None, rowvalid=None,
                  post=None, no_pv=False):
    qb_, q_ap = qT
    sc, mx, rs, P, tmp = ac.sc, ac.mx, ac.rs, ac.P, ac.tmp
    far_ids = [i for i, c in enumerate(chunks) if c["near"] is None]
    near_ids = [i for i, c in enumerate(chunks) if c["near"] is not None]
    toff = {}
    o = 0
    for i in near_ids:
        toff[i] = o
        o += chunks[i]["kn"]

    def qk(ch, bank):
        kb_, k_ap = ch["kT"]
        kn = ch["kn"]
        fw.op("pe", lambda e: e.matmul(bank[:, 0:kn], lhsT=q_ap, rhs=k_ap, start=True, stop=(ch["aug"] is None)),
              reads=list(qb_) + list(kb_), writes=[bank])
        if ch["aug"] is not None:
            ab_, l_ap, r_ap = ch["aug"]
            fw.op("pe", lambda e: e.matmul(bank[:, 0:kn], lhsT=l_ap, rhs=r_ap, start=False, stop=True),
                  reads=list(ab_), writes=[bank])

    for i, ch in enumerate(chunks):
        ac.si += 1
        bank = ac.ps_s[ac.si % 2]
        kn = ch["kn"]
        qk(ch, bank)
        if ch["near"] is not None:
            nb_, n_ap = ch["near"]
            t_ap = tmp[:, toff[i]:toff[i] + kn]
            fw.op("dve", lambda e: e.tensor_tensor(out=t_ap, in0=bank[:, 0:kn], in1=n_ap, op=ALU.add),
                  reads=[bank] + list(nb_), writes=[tmp])
            fw.op("dve", lambda e: e.tensor_reduce(out=mx[:, i:i + 1], in_=t_ap, axis=AX.X, op=ALU.max),
                  reads=[tmp], writes=[mx])
        else:
            fw.op("dve", lambda e: e.tensor_reduce(out=mx[:, i:i + 1], in_=bank[:, 0:kn], axis=AX.X, op=ALU.max),
                  reads=[bank], writes=[mx])
    n = len(chunks)
    nf = len(far_ids)
    if nf:
        fw.op("dve", lambda e: e.tensor_reduce(out=sc[:, 0:1], in_=mx[:, 0:nf], axis=AX.X, op=ALU.max), reads=[mx], writes=[sc])
        if c_far is not None:
            cb_, c_ap = c_far
            fw.op("dve", lambda e: e.tensor_tensor(out=sc[:, 0:1], in0=sc[:, 0:1], in1=c_ap, op=ALU.add),
                  reads=[sc] + list(cb_), writes=[sc])
    if n > nf:
        fw.op("dve", lambda e: e.tensor_reduce(out=sc[:, 1:2], in_=mx[:, nf:n], axis=AX.X, op=ALU.max), reads=[mx], writes=[sc])
    if nf and n > nf:
        fw.op("dve", lambda e: e.tensor_tensor(out=sc[:, 2:3], in0=sc[:, 0:1], in1=sc[:, 1:2], op=ALU.max), reads=[sc], writes=[sc])
    elif nf:
        fw.op("dve", lambda e: e.tensor_copy(out=sc[:, 2:3], in_=sc[:, 0:1]), reads=[sc], writes=[sc])
    else:
        fw.op("dve", lambda e: e.tensor_copy(out=sc[:, 2:3], in_=sc[:, 1:2]), reads=[sc], writes=[sc])
    fw.op("dve", lambda e: e.tensor_scalar(out=sc[:, 3:4], in0=sc[:, 2:3], scalar1=-1.0, scalar2=None, op0=ALU.mult),
          reads=[sc], writes=[sc])
    if c_far is not None:
        cb_, c_ap = c_far
        fw.op("dve", lambda e: e.tensor_tensor(out=sc[:, 4:5], in0=sc[:, 3:4], in1=c_ap, op=ALU.add),
              reads=[sc] + list(cb_), writes=[sc])
    for i, ch in enumerate(chunks):
        kn, k0 = ch["kn"], ch["k0"]
        if ch["near"] is not None:
            t_ap = tmp[:, toff[i]:toff[i] + kn]
            fw.op("act", lambda e: e.activation(out=P[:, k0:k0 + kn], in_=t_ap, func=AF.Exp, bias=sc[:, 3:4],
                                                accum_out=rs[:, i:i + 1]), reads=[tmp, sc], writes=[P, rs])
        else:
            ac.si += 1
            bank = ac.ps_s[ac.si % 2]
            qk(ch, bank)
            b_ap = sc[:, 4:5] if c_far is not None else sc[:, 3:4]
            fw.op("act", lambda e: e.activation(out=P[:, k0:k0 + kn], in_=bank[:, 0:kn], func=AF.Exp, bias=b_ap,
                                                accum_out=rs[:, i:i + 1]), reads=[bank, sc], writes=[P, rs])
    fw.op("dve", lambda e: e.tensor_reduce(out=sc[:, 5:6], in_=rs[:, 0:n], axis=AX.X, op=ALU.add), reads=[rs], writes=[sc])
    fw.op("dve", lambda e: e.reciprocal(out=sc[:, 6:7], in_=sc[:, 5:6]), reads=[sc], writes=[sc])
    if rowvalid is not None:
        rb_, r_ap = rowvalid
        fw.op("dve", lambda e: e.tensor_tensor(out=sc[:, 6:7], in0=sc[:, 6:7], in1=r_ap, op=ALU.mult),
              reads=[sc] + list(rb_), writes=[sc])
    if post is not None:
        post(sc[:, 6:7])
    if no_pv:
        return
    if gate is not None:
        gb_, g_ap = gate
        fw.op("dve", lambda e: e.tensor_tensor(out=sc[:, 6:7], in0=sc[:, 6:7], in1=g_ap, op=ALU.mult),
              reads=[sc] + list(gb_), writes=[sc])
    nk = chunks[-1]["k0"] + chunks[-1]["kn"]
    nblk = -(-nk // 128)
    if nk % 128:
        fw.op("dve", lambda e: e.memset(P[:, nk:nblk * 128], 0.0), writes=[P])
    for b4 in range(0, nblk, 4):
        nb = min(4, nblk - b4)
        ac.ti += 1
        bank_t, PT = ac.ps_t[ac.ti % 2], ac.PT[ac.ti % 2]
        for j in range(nb):
            fw.op("pe", lambda e, j=j: e.matmul(bank_t[:, j * 128:(j + 1) * 128], lhsT=P[:, (b4 + j) * 128:(b4 + j + 1) * 128],
                                                rhs=ac.identb[:, :], is_transpose=True, start=True, stop=True), reads=[P, ac.identb], writes=[bank_t])
        eng = "act" if ac.ti % 3 == 0 else "dve"
        fw.copy(eng, PT[:, 0:nb * 128], bank_t[:, 0:nb * 128], [bank_t], [PT])
        for j in range(nb):
            vb_, v_ap = vfn(b4 + j)
            fw.op("pe", lambda e, j=j, v_ap=v_ap: e.matmul(ac.ps_o[:, :], lhsT=PT[:, j * 128:(j + 1) * 128], rhs=v_ap,
                                                            start=(b4 + j == 0), stop=(b4 + j == nblk - 1)),
                  reads=[PT] + list(vb_), writes=[ac.ps_o])
    if first:
        fw.op("dve", lambda e: e.tensor_scalar(out=out, in0=ac.ps_o[:, :], scalar1=sc[:, 6:7], scalar2=None, op0=ALU.mult),
              reads=[ac.ps_o, sc], writes=list(out_bufs))
    else:
        fw.op("dve", lambda e: e.scalar_tensor_tensor(out=out, in0=ac.ps_o[:, :], scalar=sc[:, 6:7], in1=out,
                                                      op0=ALU.mult, op1=ALU.add), reads=[ac.ps_o, sc] + list(out_bufs),
              writes=list(out_bufs))


class Cfg:
    def __init__(self, D, T, F, depth=4):
        self.D, self.T, self.F, self.depth = D, T, F, depth
        self.H = D // 128
        self.HPC = self.H // NCORES
        self.TC = T // NCORES
        self.TT = min(512, self.TC)
        self.KC = D // 128
        self.G = 4
        self.HG = self.H // 4
        self.NSA_IN = D + 6 * 512 + 3 * self.H
        self.FOX_IN = 3 * D + self.H


class FoxX:
    def __init__(self, fw, cfg):
        H, TC, T, HPC = cfg.H, cfg.TC, cfg.T, cfg.HPC
        self.xq = [XUnit(fw, f"fxq{h}", 128, TC, BF16) for h in range(H)]
        self.xk = [XUnit(fw, f"fxk{h}", 128, TC, BF16) for h in range(H)]
        self.xv = [XUnit(fw, f"fxv{h}", TC, 128, BF16) for h in range(H)]
        self.xlf = XUnit(fw, "fxlf", H, TC, F32)
        self.lq = [fw.dram(f"flq{j}", [1024, TC], BF16) for j in range(HPC)]
        self.lk = [fw.dram(f"flk{j}", [1024, TC], BF16) for j in range(HPC)]
        self.lv = [fw.dram(f"flv{j}", [T, 128], BF16) for j in range(HPC)]
        self.llf = fw.dram("fllf", [HPC, 8 * TC], F32)
        self.augq = fw.dram("faugq", [HPC, 6, T], BF16)
        self.augk = fw.dram("faugk", [HPC, 6, T], BF16)


class OutX:
    def __init__(self, fw, cfg, name, nslot, cols):
        self.u = [[XUnit(fw, f"{name}{j}_{s}", 128, cols, BF16) for s in range(NCORES)] for j in range(nslot)]


def p1_fox(fw, pc, cfg, fx, L, h_d, consts):
    D, TC, TT, KC, H = cfg.D, cfg.TC, cfg.TT, cfg.KC, cfg.H
    nq = D // 512
    wname = f"fin{L}"
    negb = consts["negb"]
    with fw.nc.allow_non_contiguous_dma(reason="tiny bias vector"):
        fw.dma("sp", negb[0:H, 0:1], consts["b_f"].ap()[L, :].rearrange("(h o) -> h o", o=1), reads=[consts["b_f"]], writes=[negb])
    fw.op("dve", lambda e: e.tensor_scalar(out=negb[0:H, 0:1], in0=negb[0:H, 0:1], scalar1=-1.0, scalar2=None, op0=ALU.mult),
          reads=[negb], writes=[negb])
    for t0 in range(0, TC, TT):
        norm_tile(fw, pc, h_d, t0, TT)
        for ng in range(3 * nq):
            kind = ng // nq
            if kind < 2:
                def ev(c, cw, bank, ng=ng, kind=kind):
                    h = (ng % nq) * 4 + c
                    t = pc.next_evb()
                    if kind == 0:
                        fw.op("act", lambda e: e.activation(out=t[:, 0:TT], in_=bank[:, 0:TT], func=AF.Copy, scale=128 ** -0.5),
                              reads=[bank], writes=[t])
                    else:
                        fw.copy("dve", t[:, 0:TT], bank[:, 0:TT], [bank], [t])
                    unit = fx.xq[h] if kind == 0 else fx.xk[h]
                    fw.dma("act", unit.src.ap()[:, t0:t0 + TT], t[:, 0:TT], reads=[t], writes=[unit.src])
                linear_b(fw, pc, wname, ng, pc.aT, KC, 512, TT, pc.bank_set(), ev)
            else:
                def ev(s_, bank, ng=ng):
                    t = pc.next_evb()
                    fw.copy("act" if s_ % 2 else "dve", t[:, :], bank[:, :], [bank], [t])
                    for c in range(4):
                        h = (ng % nq) * 4 + c
                        fw.dma("act", fx.xv[h].src.ap()[t0 + s_ * 128:t0 + (s_ + 1) * 128, :], t[:, c * 128:(c + 1) * 128],
                               reads=[t], writes=[fx.xv[h].src])
                linear_a(fw, pc, wname, ng, pc.aT, KC, 512, TT, pc.bank_set(), ev)

        def ev_lf(c, cw, bank):
            t = pc.next_ev()
            fw.op("dve", lambda e: e.tensor_scalar(out=t[0:H, 0:TT], in0=bank[0:H, 0:TT], scalar1=-1.0, scalar2=negb[0:H, 0:1],
                                                   op0=ALU.mult, op1=ALU.add), reads=[bank, negb], writes=[t])
            fw.op("act", lambda e: e.activation(out=t[0:H, 0:TT], in_=t[0:H, 0:TT], func=AF.Exp), reads=[t], writes=[t])
            fw.op("dve", lambda e: e.tensor_scalar(out=t[0:H, 0:TT], in0=t[0:H, 0:TT], scalar1=1.0, scalar2=None, op0=ALU.add),
                  reads=[t], writes=[t])
            fw.op("act", lambda e: e.activation(out=t[0:H, 0:TT], in_=t[0:H, 0:TT], func=AF.Ln), reads=[t], writes=[t])
            fw.dma("act", fx.xlf.src.ap()[:, t0:t0 + TT], t[0:H, 0:TT], reads=[t], writes=[fx.xlf.src])
        linear_b(fw, pc, wname, 3 * nq, pc.aT, KC, H, TT, pc.bank_set(), ev_lf)


def fox_exchange(fw, cfg, fx):
    H, HPC, TC = cfg.H, cfg.HPC, cfg.TC
    for h in range(H):
        fx.xq[h].gather(fw)
        fx.xk[h].gather(fw)
        fx.xv[h].gather(fw)
    fx.xlf.gather(fw)
    per_core = []
    reads, writes = [fx.xlf.full], [fx.llf]
    for c in range(NCORES):
        lst = []
        for j in range(HPC):
            h = c * HPC + j
            lst.append((fx.lq[j].ap()[:, :], fx.xq[h].full.ap()[:, :]))
            lst.append((fx.lk[j].ap()[:, :], fx.xk[h].full.ap()[:, :]))
            lst.append((fx.lv[j].ap()[:, :], fx.xv[h].full.ap()[:, :]))
            lst.append((fx.llf.ap()[j:j + 1, :].rearrange("o (r t) -> o r t", r=8),
                        fx.xlf.full.ap().rearrange("(r h) t -> h r t", h=H)[h:h + 1, :, :]))
        per_core.append(lst)
    for h in range(H):
        reads += [fx.xq[h].full, fx.xk[h].full, fx.xv[h].full]
    for j in range(HPC):
        writes += [fx.lq[j], fx.lk[j], fx.lv[j]]
    fw.switch_dmas(per_core, reads, writes)


def fox_prep(fw, st, cfg, fx):
    HPC, T = cfg.HPC, cfg.T
    SEG = min(2048, T)
    sp_t = fw.sbuf(st, "fp_sp", [HPC, SEG], F32)
    cs_t = fw.sbuf(st, "fp_cs", [HPC, SEG], F32)
    ones_t = fw.sbuf(st, "fp_one", [HPC, SEG], F32)
    pb = [fw.sbuf(st, f"fp_pb{i}", [HPC, SEG], BF16) for i in range(3)]
    nb = [fw.sbuf(st, f"fp_nb{i}", [HPC, SEG], BF16) for i in range(3)]
    oneb = fw.sbuf(st, "fp_oneb", [HPC, 3, SEG], BF16)
    carry = fw.sbuf(st, "fp_carry", [HPC, 1], F32)
    fw.op("dve", lambda e: e.memset(ones_t[:, :], 1.0), writes=[ones_t])
    fw.op("dve", lambda e: e.memset(oneb[:, :, :], 1.0), writes=[oneb])
    fw.op("dve", lambda e: e.memset(carry[:, :], 0.0), writes=[carry])
    for s0 in range(0, T, SEG):
        fw.dma("sp", sp_t[:, :], fx.llf.ap()[:, s0:s0 + SEG], reads=[fx.llf], writes=[sp_t])
        fw.op("dve", lambda e: e.tensor_tensor_scan(out=cs_t[:, :], data0=ones_t[:, :], data1=sp_t[:, :], initial=carry[:, 0:1],
                                                    op0=ALU.mult, op1=ALU.add), reads=[ones_t, sp_t, carry], writes=[cs_t])
        fw.op("dve", lambda e: e.tensor_copy(out=carry[:, :], in_=cs_t[:, SEG - 1:SEG]), reads=[cs_t], writes=[carry])
        for i in range(3):
            fw.op("dve", lambda e, i=i: e.tensor_copy(out=pb[i][:, :], in_=cs_t[:, :]), reads=[cs_t], writes=[pb[i]])
            if i < 2:
                fw.op("dve", lambda e, i=i: e.tensor_tensor(out=cs_t[:, :], in0=cs_t[:, :], in1=pb[i][:, :], op=ALU.subtract),
                      reads=[cs_t, pb[i]], writes=[cs_t])
            fw.op("dve", lambda e, i=i: e.tensor_scalar(out=nb[i][:, :], in0=pb[i][:, :], scalar1=-1.0, scalar2=None, op0=ALU.mult),
                  reads=[pb[i]], writes=[nb[i]])
            fw.dma("sp", fx.augk.ap()[:, 3 + i, s0:s0 + SEG], pb[i][:, :], reads=[pb[i]], writes=[fx.augk])
            fw.dma("sp", fx.augq.ap()[:, i, s0:s0 + SEG], nb[i][:, :], reads=[nb[i]], writes=[fx.augq])
        fw.dma("sp", fx.augk.ap()[:, 0:3, s0:s0 + SEG], oneb[:, :, :], reads=[oneb], writes=[fx.augk])
        fw.dma("sp", fx.augq.ap()[:, 3:6, s0:s0 + SEG], oneb[:, :, :], reads=[oneb], writes=[fx.augq])


def fox_attention(fw, cfg, fx, ox, consts):
    T, TC, HPC = cfg.T, cfg.TC, cfg.HPC
    with ExitStack() as st:
        fox_prep(fw, st, cfg, fx)
    fw.barrier()
    with ExitStack() as st:
        ac = ACtx(fw, st, T, 128, consts["identb"])
        KT = fw.sbuf(st, "fKT", [128, T], BF16)
        V = fw.sbuf(st, "fV", [128, T // 128, 128], BF16)
        AK = fw.sbuf(st, "fAK", [6, T], BF16)
        qt = [fw.sbuf(st, f"fq{i}", [128, 128], BF16) for i in range(2)]
        aq = [fw.sbuf(st, f"faq{i}", [6, 128], BF16) for i in range(2)]
        osb = [fw.sbuf(st, f"fo{i}", [128, 128], F32) for i in range(2)]
        otb = [fw.sbuf(st, f"fot{i}", [128, 128], BF16) for i in range(2)]
        tri = fw.sbuf(st, "ftri", [128, 128], F32)
        identf = fw.sbuf(st, "fidf", [128, 128], F32)
        fw.dma("sp", tri[:, :], consts["tri"].ap()[:, :], reads=[consts["tri"]], writes=[tri])
        fw.dma("sp", identf[:, :], consts["identf"].ap()[:, :], reads=[consts["identf"]], writes=[identf])
        for j in range(HPC):
            fw.dma("sp", KT[:, :].rearrange("d (r t) -> d r t", r=8), fx.lk[j].ap().rearrange("(r d) t -> d r t", d=128),
                   reads=[fx.lk[j]], writes=[KT])
            fw.dma("sp", V[:, :, :], fx.lv[j].ap().rearrange("(b p) d -> p b d", p=128), reads=[fx.lv[j]], writes=[V])
            fw.dma("sp", AK[:, :], fx.augk.ap()[j, :, :], reads=[fx.augk], writes=[AK])
            for qb in range(T // 128):
                t0 = qb * 128
                r, tl = t0 // TC, t0 % TC
                q, a, o, ot = qt[qb % 2], aq[qb % 2], osb[qb % 2], otb[qb % 2]
                fw.dma("sp", q[:, :], fx.lq[j].ap()[r * 128:(r + 1) * 128, tl:tl + 128], reads=[fx.lq[j]], writes=[q])
                fw.dma("sp", a[:, :], fx.augq.ap()[j, :, t0:t0 + 128], reads=[fx.augq], writes=[a])
                chunks = []
                for k0 in range(0, t0, 512):
                    kn = min(512, t0 - k0)
                    chunks.append(dict(k0=k0, kn=kn, kT=([KT], KT[:, k0:k0 + kn]), aug=([a, AK], a[:, :], AK[:, k0:k0 + kn]), near=None))
                chunks.append(dict(k0=t0, kn=128, kT=([KT], KT[:, t0:t0 + 128]), aug=([a, AK], a[:, :], AK[:, t0:t0 + 128]),
                                   near=([tri], tri[:, :])))
                attn_rowblock(fw, ac, ([q], q[:, :]), chunks, lambda b: ([V], V[:, b, :]), o[:, :], [o], True)
                bank = ac.ps_s[qb % 2]
                fw.op("pe", lambda e: e.matmul(bank[:, 0:128], lhsT=o[:, :], rhs=identf[:, :], is_transpose=True, start=True, stop=True), reads=[o, identf], writes=[bank])
                fw.copy("act", ot[:, :], bank[:, 0:128], [bank], [ot])
                u = ox.u[j][r]
                fw.dma("act", u.src.ap()[:, tl:tl + 128], ot[:, :], reads=[ot], writes=[u.src])


def p2_phase(fw, pc, cfg, ox, nslot, L, woname, h_in, h_mid, h_out, consts, loader):
    TC, TT, KC = cfg.TC, cfg.TT, cfg.KC
    load_gainT(fw, pc, consts["norm_ffn"], L)
    for t0 in range(0, TC, TT):
        loader(t0)
        residual_linear_a(fw, pc, woname, pc.aT, KC, h_in, h_mid, t0, TT)
        norm_tile(fw, pc, h_mid, t0, TT)
        ffn_phase(fw, pc, L, h_mid, h_out, t0, TT)


def o_localize(fw, cfg, ox, lo):
    HPC = cfg.HPC
    per_core, reads = [], []
    for s_ in range(NCORES):
        lst = []
        for j in range(HPC):
            u = ox.u[j][s_]
            lst.append((lo[j].ap()[:, :], u.full.ap()[:, :]))
            reads.append(u.full)
        per_core.append(lst)
    fw.switch_dmas(per_core, reads, list(lo))


def fox_o_loader(fw, pc, cfg, lo):
    HPC, KC, TT = cfg.HPC, cfg.KC, cfg.TT

    def loader(t0):
        for j in range(HPC):
            fw.dma("sp", pc.aT[:, j:KC:HPC, 0:TT], lo[j].ap().rearrange("(r d) t -> d r t", d=128)[:, :, t0:t0 + TT],
                   reads=[lo[j]], writes=[pc.aT])
    return loader


class NsaX:
    def __init__(self, fw, cfg):
        H, TC, T, HG, G, HPC = cfg.H, cfg.TC, cfg.T, cfg.HG, cfg.G, cfg.HPC
        self.xq = [XUnit(fw, f"nxq{h}", 128, TC, BF16) for h in range(H)]
        self.xkv = {}
        for j6 in range(6):
            for g in range(G):
                if j6 in (3, 5):
                    self.xkv[(j6, g)] = XUnit(fw, f"nxkv{j6}_{g}", TC, 128, BF16)
                else:
                    self.xkv[(j6, g)] = XUnit(fw, f"nxkv{j6}_{g}", 128, TC, BF16)
        self.xg = [XUnit(fw, f"nxg{c}", TC, 3 * HPC, F32) for c in range(NCORES)]
        self.lq = [fw.dram(f"nlq{j}", [1024, TC], BF16) for j in range(2 * HPC)]
        self.lkv = {}
        for j6 in range(6):
            self.lkv[j6] = fw.dram(f"nlkv{j6}", [T, 128] if j6 in (3, 5) else [1024, TC], BF16)
        self.lg = fw.dram("nlg", [T, 3 * HPC], F32)


def p1_nsa(fw, pc, cfg, nx, L, h_d):
    D, TC, TT, KC, H, HG, G = cfg.D, cfg.TC, cfg.TT, cfg.KC, cfg.H, cfg.HG, cfg.G
    nq = D // 512
    wname = f"nin{L}"
    for t0 in range(0, TC, TT):
        norm_tile(fw, pc, h_d, t0, TT)
        for ng in range(nq + 6):
            if ng < nq or (ng - nq) in (0, 1, 2, 4):
                def ev(c, cw, bank, ng=ng):
                    t = pc.next_evb()
                    if ng < nq:
                        unit = nx.xq[ng * 4 + c]
                        fw.op("act", lambda e: e.activation(out=t[:, 0:TT], in_=bank[:, 0:TT], func=AF.Copy, scale=128 ** -0.5),
                              reads=[bank], writes=[t])
                    else:
                        unit = nx.xkv[(ng - nq, c)]
                        fw.copy("dve", t[:, 0:TT], bank[:, 0:TT], [bank], [t])
                    fw.dma("act", unit.src.ap()[:, t0:t0 + TT], t[:, 0:TT], reads=[t], writes=[unit.src])
                linear_b(fw, pc, wname, ng, pc.aT, KC, 512, TT, pc.bank_set(), ev)
            else:
                def ev(s_, bank, ng=ng):
                    t = pc.next_evb()
                    fw.copy("act" if s_ % 2 else "dve", t[:, :], bank[:, :], [bank], [t])
                    for c in range(4):
                        u = nx.xkv[(ng - nq, c)]
                        fw.dma("act", u.src.ap()[t0 + s_ * 128:t0 + (s_ + 1) * 128, :], t[:, c * 128:(c + 1) * 128],
                               reads=[t], writes=[u.src])
                linear_a(fw, pc, wname, ng, pc.aT, KC, 512, TT, pc.bank_set(), ev)

        def ev_g(s_, bank):
            t = pc.next_ev()
            fw.op("act", lambda e: e.activation(out=t[:, 0:3 * H], in_=bank[:, 0:3 * H], func=AF.Sigmoid), reads=[bank], writes=[t])
            for c8 in range(NCORES):
                fw.dma("act", nx.xg[c8].src.ap()[t0 + s_ * 128:t0 + (s_ + 1) * 128, :], t[:, c8 * 3 * cfg.HPC:(c8 + 1) * 3 * cfg.HPC],
                       reads=[t], writes=[nx.xg[c8].src])
        linear_a(fw, pc, wname, nq + 6, pc.aT, KC, 3 * H, TT, pc.bank_set(), ev_g)


def nsa_exchange(fw, cfg, nx):
    H, HG, G, HPC = cfg.H, cfg.HG, cfg.G, cfg.HPC
    reads = []
    for h in range(H):
        nx.xq[h].gather(fw)
        reads.append(nx.xq[h].full)
    for key, u in nx.xkv.items():
        u.gather(fw)
        reads.append(u.full)
    for c8 in range(NCORES):
        nx.xg[c8].gather(fw)
        reads.append(nx.xg[c8].full)
    per_core = []
    for c in range(NCORES):
        g, par = c // 2, c % 2
        lst = []
        for j in range(HPC):
            lst.append((nx.lq[j].ap()[:, :], nx.xq[HPC * c + j].full.ap()[:, :]))
            lst.append((nx.lq[HPC + j].ap()[:, :], nx.xq[HPC * (c ^ 1) + j].full.ap()[:, :]))
        for j6 in range(6):
            lst.append((nx.lkv[j6].ap()[:, :], nx.xkv[(j6, g)].full.ap()[:, :]))
        lst.append((nx.lg.ap()[:, :], nx.xg[c].full.ap()[:, :]))
        per_core.append(lst)
    writes = list(nx.lq) + [nx.lkv[j6] for j6 in range(6)] + [nx.lg]
    fw.switch_dmas(per_core, reads, writes)


def nsa_compress(fw, cfg, nx, consts, Lj, which, KcT, out_kT=None, out_v=None):
    T = cfg.T
    NC = T // 16 - 1
    NCP = -(-(T // 16) // 128) * 128
    src = nx.lkv[which]
    with ExitStack() as st:
        w1s = fw.sbuf(st, "cw1s", [128, 32, 128], F32)
        w1 = fw.sbuf(st, "cw1", [128, 32, 128], BF16)
        w2s = fw.sbuf(st, "cw2s", [128, 128], F32)
        w2 = fw.sbuf(st, "cw2", [128, 128], BF16)
        pos_s = fw.sbuf(st, "cpos_s", [128, 32], F32)
        posb = fw.sbuf(st, "cposb", [128, 32], BF16)
        c1 = fw.sbuf(st, "cc1", [128, 1], F32)
        xs = fw.sbuf(st, "cxs", [128, 512], F32)
        us = fw.sbuf(st, "cus", [128, 512], F32)
        GT = fw.sbuf(st, "cGT", [128, NCP], BF16)
        ps = [fw.psum(st, f"cps{i}", [128, 512], F32) for i in range(2)]
        fw.dma("sp", KcT[:, :].rearrange("d (r t) -> d r t", r=8), src.ap().rearrange("(r d) t -> d r t", d=128), reads=[src], writes=[KcT])
        w1_d, w2_d, pos_d = consts["cmp_w1"], consts["cmp_w2"], consts["cmp_pos"]
        fw.dma("sp", w1s[:, :, :], w1_d.ap()[Lj, which, :, :].rearrange("(j d) e -> d j e", d=128), reads=[w1_d], writes=[w1s])
        fw.copy("dve", w1[:, :, :], w1s[:, :, :], [w1s], [w1])
        fw.dma("sp", w2s[:, :], w2_d.ap()[Lj, which, :, :], reads=[w2_d], writes=[w2s])
        fw.copy("dve", w2[:, :], w2s[:, :], [w2s], [w2])
        with fw.nc.allow_non_contiguous_dma(reason="tiny positional table"):
            fw.dma("sp", pos_s[:, :], pos_d.ap()[Lj, which, :, :].rearrange("j d -> d j"), reads=[pos_d], writes=[pos_s])
        fw.copy("dve", posb[:, :], pos_s[:, :], [pos_s], [posb])
        fw.op("dve", lambda e: e.memset(GT[:, :], 0.0), writes=[GT])
        for j in range(32):
            fw.op("pe", lambda e, j=j: e.matmul(ps[0][:, 0:1], lhsT=w1[:, j, :], rhs=posb[:, j:j + 1], start=(j == 0), stop=(j == 31)),
                  reads=[w1, posb], writes=[ps[0]])
        fw.copy("dve", c1[:, :], ps[0][:, 0:1], [ps[0]], [c1])
        for i, n0 in enumerate(range(0, NC, 512)):
            cnt = min(512, NC - n0)
            bank = ps[(i + 1) % 2]
            for j in range(32):
                a0 = 16 * n0 + j
                fw.op("pe", lambda e, j=j, a0=a0: e.matmul(bank[:, 0:cnt], lhsT=w1[:, j, :], rhs=KcT[:, a0:a0 + 16 * (cnt - 1) + 1:16],
                                                           start=(j == 0), stop=(j == 31)), reads=[w1, KcT], writes=[bank])
            fw.op("act", lambda e: e.activation(out=xs[:, 0:cnt], in_=bank[:, 0:cnt], func=AF.Identity, bias=c1[:, 0:1]),
                  reads=[bank, c1], writes=[xs])
            fw.op("dve", lambda e: e.tensor_tensor(out=us[:, 0:cnt], in0=xs[:, 0:cnt], in1=xs[:, 0:cnt], op=ALU.mult), reads=[xs], writes=[us])
            fw.op("dve", lambda e: e.tensor_scalar(out=us[:, 0:cnt], in0=us[:, 0:cnt], scalar1=0.044715, scalar2=1.0, op0=ALU.mult, op1=ALU.add),
                  reads=[us], writes=[us])
            fw.op("dve", lambda e: e.tensor_tensor(out=us[:, 0:cnt], in0=us[:, 0:cnt], in1=xs[:, 0:cnt], op=ALU.mult), reads=[us, xs], writes=[us])
            fw.op("act", lambda e: e.activation(out=us[:, 0:cnt], in_=us[:, 0:cnt], func=AF.Tanh, scale=0.7978845608028654), reads=[us], writes=[us])
            fw.op("dve", lambda e: e.scalar_tensor_tensor(out=us[:, 0:cnt], in0=us[:, 0:cnt], scalar=1.0, in1=xs[:, 0:cnt], op0=ALU.add, op1=ALU.mult),
                  reads=[us, xs], writes=[us])
            fw.op("act", lambda e: e.activation(out=GT[:, n0:n0 + cnt], in_=us[:, 0:cnt], func=AF.Copy, scale=0.5), reads=[us], writes=[GT])
        if out_kT is not None:
            for i, n0 in enumerate(range(0, NCP, 512)):
                cnt = min(512, NCP - n0)
                bank = ps[i % 2]
                fw.op("pe", lambda e: e.matmul(bank[:, 0:cnt], lhsT=w2[:, :], rhs=GT[:, n0:n0 + cnt], start=True, stop=True), reads=[w2, GT], writes=[bank])
                fw.copy("dve", out_kT[:, n0:n0 + cnt], bank[:, 0:cnt], [bank], [out_kT])
        else:
            for b in range(NCP // 128):
                bank = ps[b % 2]
                fw.op("pe", lambda e, b=b: e.matmul(bank[:, 0:128], lhsT=GT[:, b * 128:(b + 1) * 128], rhs=w2[:, :], start=True, stop=True), reads=[w2, GT], writes=[bank])
                fw.copy("dve", out_v[:, b, :], bank[:, 0:128], [bank], [out_v])
    fw.barrier()


def nsa_attention(fw, cfg, nx, ox, consts, Lj):
    T, TC, HPC = cfg.T, cfg.TC, cfg.HPC
    NC = T // 16 - 1
    NCP = -(-(T // 16) // 128) * 128
    NS = T // 64
    NSP = max(NS, 128)
    nhalf = -(-NS // 128)
    NH2 = 2 * HPC
    with ExitStack() as st0:
        KCT = fw.sbuf(st0, "nKCT", [128, NCP], BF16)
        VC = fw.sbuf(st0, "nVC", [128, NCP // 128, 128], BF16)
        KsT = fw.sbuf(st0, "nKsT", [128, T], BF16)
        nsa_compress(fw, cfg, nx, consts, Lj, 0, KsT, out_kT=KCT)
        nsa_compress(fw, cfg, nx, consts, Lj, 1, KsT, out_v=VC)
        with ExitStack() as st:
            ac = ACtx(fw, st, T, 640, consts["identb"])
            VS = fw.sbuf(st, "nVS", [128, T // 128, 128], BF16)
            WIN = fw.sbuf(st, "nWIN", [128, HPC, 640], F32)
            NEARC = fw.sbuf(st, "nNEARC", [128, NH2, 24], F32)
            C31 = fw.sbuf(st, "nC31", [128, NH2], F32)
            FPAT = fw.sbuf(st, "nFPAT", [128, 4], F32)
            RV = fw.sbuf(st, "nRV", [128, 1], F32)
            WSEL = fw.sbuf(st, "nWSEL", [128, 8192], BF16)
            identf = fw.sbuf(st, "nidf", [128, 128], F32)
            PH = fw.sbuf(st, "nPH", [128, NCP + 16], F32)
            score = fw.sbuf(st, "nscore", [128, NSP], F32)
            score2 = fw.sbuf(st, "nscore2", [128, NSP], F32)
            m8 = fw.sbuf(st, "nm8", [128, 16], F32)
            MB = fw.sbuf(st, "nMB", [128, NSP], BF16)
            MBT = fw.sbuf(st, "nMBT", [128, nhalf, 128], BF16)
            GTl = [fw.sbuf(st, f"nG{i}", [128, 3 * HPC], F32) for i in range(2)]
            qt = [[fw.sbuf(st, f"nq{i}_{hj}", [128, 128], BF16) for hj in range(NH2)] for i in range(2)]
            oacc = [fw.sbuf(st, f"noa{hj}", [128, 128], F32) for hj in range(HPC)]
            otb = [fw.sbuf(st, f"not{i}", [128, 128], BF16) for i in range(2)]
            kw = [fw.sbuf(st, f"nkw{i}", [128, 640], BF16) for i in range(2)]
            vw = [fw.sbuf(st, f"nvw{i}", [128, 5, 128], BF16) for i in range(2)]
            for name, tile_ in (("win", WIN), ("nearc", NEARC), ("c31", C31), ("fpat", FPAT), ("rv", RV), ("wsel", WSEL), ("identf", identf)):
                d = consts[name]
                if len(tile_.t.shape) == 3:
                    fw.dma("sp", tile_[:, :, :], d.ap()[:, :, :], reads=[d], writes=[tile_])
                else:
                    fw.dma("sp", tile_[:, :], d.ap()[:, :], reads=[d], writes=[tile_])
            fw.dma("sp", KsT[:, :].rearrange("d (r t) -> d r t", r=8), nx.lkv[2].ap().rearrange("(r d) t -> d r t", d=128), reads=[nx.lkv[2]], writes=[KsT])
            fw.dma("sp", VS[:, :, :], nx.lkv[3].ap().rearrange("(b p) d -> p b d", p=128), reads=[nx.lkv[3]], writes=[VS])
            fw.op("dve", lambda e: e.memset(score[:, :], -1.0), writes=[score])
            for qb in range(T // 128):
                t0 = qb * 128
                r, tl = t0 // TC, t0 % TC
                G_ = GTl[qb % 2]
                qs = qt[qb % 2]
                fw.dma("sp", G_[:, :], nx.lg.ap()[t0:t0 + 128, :], reads=[nx.lg], writes=[G_])
                for hj in range(NH2):
                    fw.dma("sp", qs[hj][:, :], nx.lq[hj].ap()[r * 128:(r + 1) * 128, tl:tl + 128], reads=[nx.lq[hj]], writes=[qs[hj]])
                wb0 = max(0, qb - 4)
                nwb = qb - wb0 + 1
                kwt, vwt = kw[qb % 2], vw[qb % 2]
                for b in range(nwb):
                    tb = (wb0 + b) * 128
                    rb, tlb = tb // TC, tb % TC
                    fw.dma("sp", kwt[:, b * 128:(b + 1) * 128], nx.lkv[4].ap()[rb * 128:(rb + 1) * 128, tlb:tlb + 128], reads=[nx.lkv[4]], writes=[kwt])
                fw.dma("sp", vwt[:, 0:nwb, :], nx.lkv[5].ap()[wb0 * 128:(qb + 1) * 128, :].rearrange("(b p) d -> p b d", p=128),
                       reads=[nx.lkv[5]], writes=[vwt])
                nkc = min(8 * qb + 7, NC)
                lo = max(0, 8 * qb - 16)
                fw.op("dve", lambda e: e.memset(PH[:, :], 0.0), writes=[PH])
                for hj in range(NH2):
                    chunks = []
                    for k0 in range(0, lo, 512):
                        kn = min(512, lo - k0)
                        chunks.append(dict(k0=k0, kn=kn, kT=([KCT], KCT[:, k0:k0 + kn]), aug=None, near=None))
                    off = lo - (8 * qb - 16)
                    chunks.append(dict(k0=lo, kn=nkc - lo, kT=([KCT], KCT[:, lo:nkc]), aug=None,
                                       near=([NEARC], NEARC[:, hj, off:off + nkc - lo])))

                    def post(recip, hj=hj):
                        if hj == 0:
                            fw.op("dve", lambda e: e.tensor_scalar(out=PH[:, 0:nkc], in0=ac.P[:, 0:nkc], scalar1=recip, scalar2=None, op0=ALU.mult),
                                  reads=[ac.P, ac.sc], writes=[PH])
                        else:
                            fw.op("dve", lambda e: e.scalar_tensor_tensor(out=PH[:, 0:nkc], in0=ac.P[:, 0:nkc], scalar=recip, in1=PH[:, 0:nkc],
                                                                          op0=ALU.mult, op1=ALU.add), reads=[ac.P, ac.sc, PH], writes=[PH])
                    own = hj < HPC
                    attn_rowblock(fw, ac, ([qs[hj]], qs[hj][:, :]), chunks, lambda b: ([VC], VC[:, b, :]),
                                  oacc[hj][:, :] if own else None, [oacc[hj]] if own else [], True,
                                  gate=([G_], G_[:, hj * 3:hj * 3 + 1]) if own else None,
                                  c_far=([C31], C31[:, hj:hj + 1]), rowvalid=([RV], RV[:, 0:1]) if qb == 0 else None,
                                  post=post, no_pv=not own)
                S_used = min(NS, 2 * qb + 2)
                fw.op("dve", lambda e: e.tensor_reduce(out=score[:, 0:S_used], in_=PH[:, 0:4 * S_used].rearrange("p (s f) -> p s f", f=4),
                                                       axis=AX.X, op=ALU.add), reads=[PH], writes=[score])
                if S_used > 1:
                    fw.op("dve", lambda e: e.tensor_tensor(out=score[:, 1:S_used], in0=score[:, 1:S_used], in1=PH[:, 3:4 * S_used - 4:4], op=ALU.add),
                          reads=[PH, score], writes=[score])
                fw.op("dve", lambda e: e.memset(score[:, 0:1], 1e6), writes=[score])
                if qb == 0:
                    fw.op("dve", lambda e: e.tensor_tensor(out=score[:, 0:2], in0=score[:, 0:2], in1=FPAT[:, 1:3], op=ALU.max), reads=[score, FPAT], writes=[score])
                else:
                    fw.op("dve", lambda e: e.tensor_tensor(out=score[:, 2 * qb - 1:2 * qb + 2], in0=score[:, 2 * qb - 1:2 * qb + 2], in1=FPAT[:, 0:3], op=ALU.max),
                          reads=[score, FPAT], writes=[score])
                fw.op("dve", lambda e: e.max(out=m8[:, 0:8], in_=score[:, 0:NSP]), reads=[score], writes=[m8])
                fw.op("dve", lambda e: e.match_replace(out=score2[:, 0:NSP], in_to_replace=m8[:, 0:8], in_values=score[:, 0:NSP], imm_value=-2.0),
                      reads=[score, m8], writes=[score2])
                fw.op("dve", lambda e: e.max(out=m8[:, 8:16], in_=score2[:, 0:NSP]), reads=[score2], writes=[m8])
                fw.op("dve", lambda e: e.tensor_scalar(out=score2[:, 0:NSP], in0=score[:, 0:NSP], scalar1=m8[:, 15:16], scalar2=None, op0=ALU.is_ge),
                      reads=[score, m8], writes=[score2])
                fw.op("dve", lambda e: e.tensor_scalar(out=MB[:, 0:NSP], in0=score2[:, 0:NSP], scalar1=1e30, scalar2=-1e30, op0=ALU.mult, op1=ALU.add),
                      reads=[score2], writes=[MB])
                for hf in range(nhalf):
                    ac.ti += 1
                    bank_t = ac.ps_t[ac.ti % 2]
                    fw.op("pe", lambda e, hf=hf: e.matmul(bank_t[:, 0:128], lhsT=MB[:, hf * 128:(hf + 1) * 128], rhs=ac.identb[:, :], is_transpose=True,
                                                          start=True, stop=True), reads=[MB, ac.identb], writes=[bank_t])
                    fw.copy("dve", MBT[:, hf, :], bank_t[:, 0:128], [bank_t], [MBT])
                for hj in range(HPC):
                    chunks = []
                    far_end = max(0, t0 - 128)
                    for k0 in range(0, far_end, 512):
                        kn = min(512, far_end - k0)
                        sb = k0 // 64
                        chunks.append(dict(k0=k0, kn=kn, kT=([KsT], KsT[:, k0:k0 + kn]),
                                           aug=([MBT, WSEL], MBT[:, sb // 128, :], WSEL[:, 64 * (sb % 128):64 * (sb % 128) + kn]), near=None))
                    for k0 in range(far_end, t0 + 128, 128):
                        sb = k0 // 64
                        woff = 640 - (t0 + 128 - k0)
                        chunks.append(dict(k0=k0, kn=128, kT=([KsT], KsT[:, k0:k0 + 128]),
                                           aug=([MBT, WSEL], MBT[:, sb // 128, :], WSEL[:, 64 * (sb % 128):64 * (sb % 128) + 128]),
                                           near=([WIN], WIN[:, hj, woff:woff + 128])))
                    attn_rowblock(fw, ac, ([qs[hj]], qs[hj][:, :]), chunks, lambda b: ([VS], VS[:, b, :]), oacc[hj][:, :], [oacc[hj]], False,
                                  gate=([G_], G_[:, hj * 3 + 1:hj * 3 + 2]), c_far=([C31], C31[:, hj:hj + 1]))
                    chunks = []
                    nkw = nwb * 128
                    for k0 in range(0, nkw, 512):
                        kn = min(512, nkw - k0)
                        woff = 640 - nkw + k0
                        chunks.append(dict(k0=k0, kn=kn, kT=([kwt], kwt[:, k0:k0 + kn]), aug=None, near=([WIN], WIN[:, hj, woff:woff + kn])))
                    attn_rowblock(fw, ac, ([qs[hj]], qs[hj][:, :]), chunks, lambda b: ([vwt], vwt[:, b, :]), oacc[hj][:, :], [oacc[hj]], False,
                                  gate=([G_], G_[:, hj * 3 + 2:hj * 3 + 3]))
                    bank = ac.ps_s[hj % 2]
                    ot = otb[hj % 2]
                    fw.op("pe", lambda e, hj=hj: e.matmul(bank[:, 0:128], lhsT=oacc[hj][:, :], rhs=identf[:, :], is_transpose=True, start=True, stop=True),
                          reads=[oacc[hj], identf], writes=[bank])
                    fw.copy("act", ot[:, :], bank[:, 0:128], [bank], [ot])
                    u = ox.u[hj][r]
                    fw.dma("act", u.src.ap()[:, tl:tl + 128], ot[:, :], reads=[ot], writes=[u.src])
        fw.barrier()


def _rel_bucket_np(dist):
    n = np.maximum(dist, 0)
    nf = np.maximum(n, 16).astype(np.float32)
    large = 16 + (np.log(nf / np.float32(16)) / np.float32(np.log(128 / 16)) * np.float32(16)).astype(np.int32)
    large = np.minimum(large, 31)
    return np.where(n < 16, n, large)


def host_nsa_tables(rel_table, H):
    tab = np.asarray(rel_table, np.float32)
    r = np.arange(128)[:, None]
    c = np.arange(640)[None, :]
    rel = r - c + 512
    band = (rel >= 0) & (rel < 512)
    win = np.where(band[None], tab[_rel_bucket_np(rel)].transpose(2, 0, 1), np.float32(NEG)).astype(np.float32)
    npr = np.arange(24)[None, :]
    dist = r + 225 - 16 * npr
    nearc = np.where((dist >= 0)[None], tab[_rel_bucket_np(dist)].transpose(2, 0, 1), np.float32(NEG)).astype(np.float32)
    c31 = tab[31]
    return win, nearc, c31


def host_consts():
    tri = np.where(np.arange(128)[None, :] <= np.arange(128)[:, None], 0.0, NEG).astype(np.float32)
    fpat = np.zeros((128, 4), np.float32)
    fpat[:64, 0] = 1e6
    fpat[:64, 1] = 1e6
    fpat[64:, 1] = 1e6
    fpat[64:, 2] = 1e6
    rv = (np.arange(128) >= 31).astype(np.float32)[:, None]
    wsel = (np.arange(8192)[None, :] // 64 == np.arange(128)[:, None]).astype(np.float32).astype(ml_dtypes.bfloat16)
    return dict(tri=tri, fpat=fpat, rv=rv, wsel=wsel, identf=np.eye(128, dtype=np.float32),
                identb=np.eye(128).astype(ml_dtypes.bfloat16))


def make_plan(cfg, n_layers):
    plan = WeightPlan()
    D, F = cfg.D, cfg.F
    for L in range(n_layers):
        j = L // 2
        if L % 2 == 0:
            plan.add(f"nin{j}", D, cfg.NSA_IN)
            plan.add(f"no{j}", D, D)
        else:
            plan.add(f"fin{j}", D, cfg.FOX_IN)
            plan.add(f"fo{j}", D, D)
        plan.add(f"wg{L}", D, F)
        plan.add(f"wu{L}", D, F)
        plan.add(f"wd{L}", F, D)
    return plan


def build_program(cfg, plan, n_layers, final_norm=True):
    nc = bass.Bass("TRN2", target_bir_lowering=False)
    D, F, T, TC, TT, H, HPC = cfg.D, cfg.F, cfg.T, cfg.TC, cfg.TT, cfg.H, cfg.HPC

    def ext(name, shape, dt=F32):
        return Buf(nc.dram_tensor(name, shape, dt, kind="ExternalInput"), name)
    x = ext("x", [TC, D])
    consts = dict(norm_mix=ext("norm_mix", [4, D]), norm_ffn=ext("norm_ffn", [4, D]), norm_final=ext("norm_final", [1, D]),
                  b_f=ext("b_f", [2, H]), cmp_w1=ext("cmp_w1", [2, 2, 4096, 128]), cmp_w2=ext("cmp_w2", [2, 2, 128, 128]),
                  cmp_pos=ext("cmp_pos", [2, 2, 32, 128]), identf=ext("identf", [128, 128]), identb=ext("identb", [128, 128], BF16),
                  tri=ext("tri", [128, 128]), fpat=ext("fpat", [128, 4]), rv=ext("rv", [128, 1]), wsel=ext("wsel", [128, 8192], BF16),
                  win=ext("win", [128, HPC, 640]), nearc=ext("nearc", [128, 2 * HPC, 24]), c31=ext("c31", [128, 2 * HPC]))
    wsh = ext("wsh", [plan.n_units * 128, 1024])
    out = Buf(nc.dram_tensor("out", [TC, D], F32, kind="ExternalOutput"), "out")
    with ExitStack() as st0:
        fw = FW(nc, st0)
        first = [plan.mats[(f"nin{L // 2}" if L % 2 == 0 else f"fin{L // 2}")][0] for L in range(n_layers)]
        ends = [-(-first[L + 1] // 4) for L in range(n_layers - 1)] + [plan.n_units]
        wunits, wstage = gather_weights(fw, plan, wsh, u0=0, u1=ends[0])
        fx = FoxX(fw, cfg) if n_layers > 1 else None
        nx = NsaX(fw, cfg)
        ox = OutX(fw, cfg, "oxo", HPC, TC)
        lo = [fw.dram(f"lo{j}", [1024, TC], BF16) for j in range(HPC)]
        hbuf = [fw.dram("h_a", [TC, D], F32), fw.dram("h_b", [TC, D], F32)]
        h_mid = fw.dram("h_mid", [TC, D], F32)
        consts["negb"] = fw.sbuf(st0, "negb", [128, 1], F32)
        h_cur = x
        for L in range(n_layers):
            j = L // 2
            with ExitStack() as st:
                pc = PCtx(fw, st, D, F, TT, plan, wunits)
                fw.dma("sp", pc.ident[:, :], consts["identf"].ap()[:, :], reads=[consts["identf"]], writes=[pc.ident])
                load_gainT(fw, pc, consts["norm_mix"], L)
                if L % 2 == 0:
                    p1_nsa(fw, pc, cfg, nx, j, h_cur)
                else:
                    p1_fox(fw, pc, cfg, fx, j, h_cur, consts)
            fw.barrier()
            if L % 2 == 0:
                nsa_exchange(fw, cfg, nx)
            else:
                fox_exchange(fw, cfg, fx)
            if L + 1 < n_layers:
                gather_weights(fw, plan, wsh, wunits, wstage, ends[L], ends[L + 1])
            if L % 2 == 0:
                nsa_attention(fw, cfg, nx, ox, consts, j)
            else:
                fox_attention(fw, cfg, fx, ox, consts)
            for j_ in range(HPC):
                for s_ in range(NCORES):
                    ox.u[j_][s_].gather(fw)
            o_localize(fw, cfg, ox, lo)
            fw.barrier()
            last = (L == n_layers - 1) and not final_norm
            h_next = out if last else hbuf[L % 2]
            with ExitStack() as st:
                pc = PCtx(fw, st, D, F, TT, plan, wunits)
                fw.dma("sp", pc.ident[:, :], consts["identf"].ap()[:, :], reads=[consts["identf"]], writes=[pc.ident])
                p2_phase(fw, pc, cfg, ox, HPC, L, f"no{j}" if L % 2 == 0 else f"fo{j}", h_cur, h_mid, h_next, consts,
                         fox_o_loader(fw, pc, cfg, lo))
            fw.barrier()
            h_cur = h_next
        if final_norm:
            with ExitStack() as st:
                pc = PCtx(fw, st, D, 128, 128, plan, wunits, final=True)
                fw.dma("sp", pc.gb[:, :], consts["norm_final"].ap()[0:1, :].partition_broadcast(128), reads=[consts["norm_final"]], writes=[pc.gb])
                for t0 in range(0, TC, 128):
                    norm_tile(fw, pc, h_cur, t0, 128, out_d=out)
        fw.finish([out])
    return nc, fw.n_inst


def run_module(inputs, cfg, n_layers, final_norm=True):
    plan = make_plan(cfg, n_layers)
    nc, n_inst = build_program(cfg, plan, n_layers, final_norm)
    D, T, TC, H, HPC = cfg.D, cfg.T, cfg.TC, cfg.H, cfg.HPC
    f32 = np.float32
    W = {}
    for L in range(n_layers):
        j = L // 2
        if L % 2 == 0:
            W[f"nin{j}"] = np.asarray(inputs["nsa_w_in"][j], f32)
            W[f"no{j}"] = np.asarray(inputs["nsa_w_o"][j], f32)
        else:
            W[f"fin{j}"] = np.asarray(inputs["fox_w_in"][j], f32)
            W[f"fo{j}"] = np.asarray(inputs["fox_w_o"][j], f32)
        W[f"wg{L}"] = np.asarray(inputs["ffn_w_gate"][L], f32)
        W[f"wu{L}"] = np.asarray(inputs["ffn_w_up"][L], f32)
        W[f"wd{L}"] = np.asarray(inputs["ffn_w_down"][L], f32)
    wsh = plan.host_pack(W)
    del W
    hc = host_consts()
    win, nearc, c31 = host_nsa_tables(inputs["rel_table"], H)
    x = np.asarray(inputs["x"], f32).reshape(T, D)
    maps = []
    for c in range(NCORES):
        own = [HPC * c + j for j in range(HPC)]
        oth = [HPC * (c ^ 1) + j for j in range(HPC)]
        m = dict(x=np.ascontiguousarray(x[c * TC:(c + 1) * TC]),
                 norm_mix=np.asarray(inputs["norm_mix"], f32), norm_ffn=np.asarray(inputs["norm_ffn"], f32),
                 norm_final=np.asarray(inputs["norm_final"], f32).reshape(1, D), b_f=np.asarray(inputs["fox_b_f"], f32),
                 cmp_w1=np.asarray(inputs["nsa_cmp_w1"], f32), cmp_w2=np.asarray(inputs["nsa_cmp_w2"], f32),
                 cmp_pos=np.asarray(inputs["nsa_cmp_pos"], f32), wsh=wsh[c],
                 win=np.ascontiguousarray(win[own].transpose(1, 0, 2)),
                 nearc=np.ascontiguousarray(nearc[own + oth].transpose(1, 0, 2)),
                 c31=np.ascontiguousarray(np.broadcast_to(c31[own + oth][None, :], (128, 2 * HPC))).astype(f32))
        m.update(hc)
        maps.append(m)
    res = run_bass_kernel_spmd(nc, maps, core_ids=list(range(NCORES)))
    o = np.concatenate([res.results[c]["out"] for c in range(NCORES)], 0)
    return o.reshape(1, T, D).astype(f32)


def kernel(**inputs):
    cfg = Cfg(4096, 16384, 11008)
    return run_module(inputs, cfg, 4)
```
